# Optimizing a Trainium2 kernel written in Bass

```python
import math
import jax, jax.numpy as jnp
from jax import lax
import numpy as np

D_MODEL = 1024
BATCH = 16
SEQ = 2048
DEPTH = 2

HEAD_DIM = 64
RW_HEADS = D_MODEL // (2 * HEAD_DIM)
RW_DIM = RW_HEADS * HEAD_DIM
DECAY_LORA = 64
AAA_LORA = 64
GATE_LORA = 128
RW_COLS = 3 * RW_DIM + DECAY_LORA + AAA_LORA + GATE_LORA
ATT_HEADS = D_MODEL // (2 * HEAD_DIM)
ATT_DIM = ATT_HEADS * HEAD_DIM
ATT_COLS = 3 * ATT_DIM
AB_COLS = RW_COLS + ATT_COLS
MOBA_BLOCK = 256
MOBA_TOPK = 3
MOBA_Q_BLOCK = 32
NUM_BUCKETS = 32
REL_MAX_DIST = 1024
CONV_WIDTH = 3
D_FF = 2816
NORM_EPS = 1e-6
LNX_EPS = 64e-5
NEG_INF = -1e30

kernel_name = "rwkv7_moba_shortconv_convffn_hybrid"


def rms_norm(x, g):
    xf = x.astype(jnp.float32)
    y = xf * lax.rsqrt(jnp.mean(xf * xf, axis=-1, keepdims=True) + NORM_EPS)
    return (y * g.astype(jnp.float32)).astype(x.dtype)


def causal_dwconv(x, w):
    k_width, chans = w.shape
    return lax.conv_general_dilated(
        x, w[:, None, :].astype(x.dtype), window_strides=(1,), padding=[(k_width - 1, 0)],
        dimension_numbers=("NWC", "WIO", "NWC"), feature_group_count=chans)


def token_shift(p):
    return jnp.pad(p, ((0, 0), (1, 0), (0, 0)))[:, :-1]


def t5_bucket(dist):
    n = jnp.maximum(dist, 0)
    max_exact = NUM_BUCKETS // 2
    nf = jnp.maximum(n, 1).astype(jnp.float32)
    large = max_exact + (jnp.log(nf / max_exact) / math.log(REL_MAX_DIST / max_exact)
                         * (NUM_BUCKETS - max_exact)).astype(jnp.int32)
    return jnp.where(n < max_exact, n, jnp.minimum(large, NUM_BUCKETS - 1))


def rwkv7_time_mix(p, w0, w_lora_up, a0, a_lora_up, g_lora_up, k_k, k_a, r_k, lnx_w, lnx_b):
    B, T, _ = p.shape
    H, N = RW_HEADS, HEAD_DIM
    f32 = jnp.float32
    r, k, v, dw, da, dg = jnp.split(
        p, [RW_DIM, 2 * RW_DIM, 3 * RW_DIM, 3 * RW_DIM + DECAY_LORA,
            3 * RW_DIM + DECAY_LORA + AAA_LORA], axis=-1)
    logw = -jax.nn.softplus(-(w0 + jnp.tanh(dw) @ w_lora_up)) - 0.5
    a = jax.nn.sigmoid(a0 + da @ a_lora_up)
    g = jax.nn.sigmoid(dg) @ g_lora_up
    kk = (k * k_k).reshape(B, T, H, N).astype(f32)
    kk = kk * lax.rsqrt(jnp.maximum(jnp.sum(kk * kk, -1, keepdims=True), 1e-24))
    k = k * (1 + (a - 1) * k_a)
    decay = jnp.exp(-jnp.exp(logw.astype(f32)))

    def heads_tm(t):
        return t.reshape(B, T, H, N).astype(f32).transpose(1, 0, 2, 3)

    xs = (heads_tm(r), heads_tm(decay), heads_tm(k), heads_tm(v),
          kk.transpose(1, 0, 2, 3), heads_tm(a))

    def step(S, inp):
        r_t, w_t, k_t, v_t, kk_t, a_t = inp
        sa = jnp.einsum("bhvk,bhk->bhv", S, -kk_t)
        S = (S * w_t[:, :, None, :] + sa[..., None] * (kk_t * a_t)[:, :, None, :]
             + v_t[..., None] * k_t[:, :, None, :])
        return S, jnp.einsum("bhvk,bhk->bhv", S, r_t)

    _, y = lax.scan(step, jnp.zeros((B, H, N, N), f32), xs)
    y = y.transpose(1, 0, 2, 3)
    mu = jnp.mean(y, -1, keepdims=True)
    var = jnp.mean(jnp.square(y - mu), -1, keepdims=True)
    y = (y - mu) * lax.rsqrt(var + LNX_EPS) * lnx_w.reshape(H, N).astype(f32) \
        + lnx_b.reshape(H, N).astype(f32)
    rh = r.reshape(B, T, H, N).astype(f32)
    kh = k.reshape(B, T, H, N).astype(f32)
    vh = v.reshape(B, T, H, N).astype(f32)
    y = y + jnp.sum(rh * kh * r_k.astype(f32), -1, keepdims=True) * vh
    y = y.reshape(B, T, RW_DIM) * g.astype(f32)
    return y.astype(p.dtype)


def moba_attention(q, k, v, rel_bias, q_norm, k_norm):
    B, T, H, Dh = q.shape
    f32 = jnp.float32
    q = rms_norm(q, q_norm).transpose(0, 2, 1, 3)
    k = rms_norm(k, k_norm).transpose(0, 2, 1, 3)
    v = v.transpose(0, 2, 1, 3)
    nb = -(-T // MOBA_BLOCK)
    pad = nb * MOBA_BLOCK - T
    kp = jnp.pad(k, ((0, 0), (0, 0), (0, pad), (0, 0)))
    vp = jnp.pad(v, ((0, 0), (0, 0), (0, pad), (0, 0)))
    kb = kp.reshape(B, H, nb, MOBA_BLOCK, Dh)
    vb = vp.reshape(B, H, nb, MOBA_BLOCK, Dh)
    kmean = jnp.mean(kb.astype(f32), axis=3)
    gate = jnp.einsum("bhtd,bhnd->bhtn", q.astype(f32), kmean)
    qblk = jnp.arange(T) // MOBA_BLOCK
    past = jnp.arange(nb)[None, :] < qblk[:, None]
    gate = jnp.where(past, gate, NEG_INF)
    n_sel = min(MOBA_TOPK, nb)
    _, sel = lax.top_k(gate, n_sel)

    nq = T // MOBA_Q_BLOCK
    qc_all = q.reshape(B, H, nq, MOBA_Q_BLOCK, Dh).transpose(2, 0, 1, 3, 4)
    sel_all = sel.reshape(B, H, nq, MOBA_Q_BLOCK, n_sel).transpose(2, 0, 1, 3, 4)
    starts = jnp.arange(nq, dtype=jnp.int32) * MOBA_Q_BLOCK
    scale = Dh ** -0.5
    bias_tbl = rel_bias.T.astype(f32)
    head_ix = jnp.arange(H)[:, None, None, None]
    offs = jnp.arange(MOBA_BLOCK)
    gather = jax.vmap(jax.vmap(lambda blocks, ix: blocks[ix]))
    n_keys_sel = n_sel * MOBA_BLOCK

    def one_chunk(args):
        q_c, sel_c, start = args
        pos = start + jnp.arange(MOBA_Q_BLOCK)
        blk = start // MOBA_BLOCK
        k_sel = gather(kb, sel_c)
        v_sel = gather(vb, sel_c)
        s_sel = jnp.einsum("bhqd,bhqjkd->bhqjk", q_c, k_sel).astype(f32) * scale
        kpos = sel_c[..., None] * MOBA_BLOCK + offs
        s_sel = s_sel + bias_tbl[head_ix, t5_bucket(pos[:, None, None] - kpos)]
        s_sel = jnp.where((sel_c < blk)[..., None], s_sel, NEG_INF)
        k_own = lax.dynamic_slice_in_dim(kp, blk * MOBA_BLOCK, MOBA_BLOCK, axis=2)
        v_own = lax.dynamic_slice_in_dim(vp, blk * MOBA_BLOCK, MOBA_BLOCK, axis=2)
        dist_own = pos[:, None] - (blk * MOBA_BLOCK + offs)[None, :]
        s_own = jnp.einsum("bhqd,bhkd->bhqk", q_c, k_own).astype(f32) * scale \
            + bias_tbl[:, t5_bucket(dist_own)]
        s_own = jnp.where(dist_own >= 0, s_own, NEG_INF)
        logits = jnp.concatenate(
            [s_sel.reshape(B, H, MOBA_Q_BLOCK, n_keys_sel), s_own], axis=-1)
        probs = jax.nn.softmax(logits, axis=-1).astype(v.dtype)
        out = jnp.einsum("bhqk,bhqkd->bhqd", probs[..., :n_keys_sel],
                         v_sel.reshape(B, H, MOBA_Q_BLOCK, n_keys_sel, Dh)) \
            + jnp.einsum("bhqk,bhkd->bhqd", probs[..., n_keys_sel:], v_own)
        return out

    out = lax.map(one_chunk, (qc_all, sel_all, starts))
    return out.transpose(1, 0, 3, 2, 4).reshape(B, T, H * Dh)


def rwkv_moba_mixer(x, rel_bias, mix_norm, w_in, shift_mu, w0, w_lora_up, a0, a_lora_up,
                    g_lora_up, k_k, k_a, r_k, lnx_w, lnx_b, q_norm, k_norm, w_out):
    B, T, _ = x.shape
    p = rms_norm(x, mix_norm) @ w_in
    p_rw, p_att = p[..., :RW_COLS], p[..., RW_COLS:]
    p_rw = p_rw + (token_shift(p_rw) - p_rw) * shift_mu
    y_rw = rwkv7_time_mix(p_rw, w0, w_lora_up, a0, a_lora_up, g_lora_up, k_k, k_a, r_k,
                          lnx_w, lnx_b)
    q, k, v = jnp.split(p_att, 3, axis=-1)
    hs = (B, T, ATT_HEADS, HEAD_DIM)
    y_att = moba_attention(q.reshape(hs), k.reshape(hs), v.reshape(hs), rel_bias, q_norm, k_norm)
    return jnp.concatenate([y_rw, y_att.astype(y_rw.dtype)], axis=-1) @ w_out


def short_conv_mixer(x, mix_norm, w_in, conv_w, w_out):
    b_gate, c_gate, h = jnp.split(rms_norm(x, mix_norm) @ w_in, 3, axis=-1)
    return (b_gate * causal_dwconv(c_gate * h, conv_w)) @ w_out


def conv_glu_ffn(x, ffn_norm, up, conv_w, conv_b, down):
    z = causal_dwconv(rms_norm(x, ffn_norm) @ up, conv_w) + conv_b
    a, u = jnp.split(z, 2, axis=-1)
    return (jax.nn.silu(a) * u) @ down


def setup_inputs(seed: int = 0) -> dict:
    key = jax.random.key(seed)
    keys = iter(list(jax.random.split(key, 48)))
    d = D_MODEL

    def nrm(shape, scale):
        return scale * jax.random.normal(next(keys), shape, jnp.float32)

    def gain(n):
        return 1.0 + nrm((n,), 0.05)

    def uni(shape, lo, hi):
        return jax.random.uniform(next(keys), shape, jnp.float32, lo, hi)

    inp = {}
    inp["x"] = nrm((BATCH, SEQ, d), 1.0)
    inp["rel_bias"] = nrm((NUM_BUCKETS, ATT_HEADS), 0.5)
    inp["l0_mix_norm"] = gain(d)
    inp["l0_w_in"] = nrm((d, AB_COLS), d ** -0.5)
    inp["l0_shift_mu"] = uni((RW_COLS,), 0.0, 1.0)
    inp["l0_w0"] = uni((RW_DIM,), -6.0, -1.0)
    inp["l0_w_lora_up"] = nrm((DECAY_LORA, RW_DIM), 0.1)
    inp["l0_a0"] = nrm((RW_DIM,), 0.5)
    inp["l0_a_lora_up"] = nrm((AAA_LORA, RW_DIM), AAA_LORA ** -0.5)
    inp["l0_g_lora_up"] = nrm((GATE_LORA, RW_DIM), GATE_LORA ** -0.5)
    inp["l0_k_k"] = 0.85 + nrm((RW_DIM,), 0.05)
    inp["l0_k_a"] = 1.0 + nrm((RW_DIM,), 0.05)
    inp["l0_r_k"] = nrm((RW_HEADS, HEAD_DIM), 0.1)
    inp["l0_lnx_w"] = gain(RW_DIM)
    inp["l0_lnx_b"] = nrm((RW_DIM,), 0.02)
    inp["l0_q_norm"] = gain(HEAD_DIM)
    inp["l0_k_norm"] = gain(HEAD_DIM)
    inp["l0_w_out"] = nrm((RW_DIM + ATT_DIM, d), (RW_DIM + ATT_DIM) ** -0.5)
    inp["l0_ffn_norm"] = gain(d)
    inp["l0_ffn_up"] = nrm((d, 2 * D_FF), d ** -0.5)
    inp["l0_ffn_conv_w"] = nrm((CONV_WIDTH, 2 * D_FF), CONV_WIDTH ** -0.5)
    inp["l0_ffn_conv_b"] = nrm((2 * D_FF,), 0.02)
    inp["l0_ffn_down"] = nrm((D_FF, d), D_FF ** -0.5)
    inp["l1_mix_norm"] = gain(d)
    inp["l1_w_in"] = nrm((d, 3 * d), d ** -0.5)
    inp["l1_conv_w"] = nrm((CONV_WIDTH, d), CONV_WIDTH ** -0.5)
    inp["l1_w_out"] = nrm((d, d), d ** -0.5)
    inp["l1_ffn_norm"] = gain(d)
    inp["l1_ffn_up"] = nrm((d, 2 * D_FF), d ** -0.5)
    inp["l1_ffn_conv_w"] = nrm((CONV_WIDTH, 2 * D_FF), CONV_WIDTH ** -0.5)
    inp["l1_ffn_conv_b"] = nrm((2 * D_FF,), 0.02)
    inp["l1_ffn_down"] = nrm((D_FF, d), D_FF ** -0.5)
    return inp


def reference(x, rel_bias,
              l0_mix_norm, l0_w_in, l0_shift_mu, l0_w0, l0_w_lora_up, l0_a0, l0_a_lora_up,
              l0_g_lora_up, l0_k_k, l0_k_a, l0_r_k, l0_lnx_w, l0_lnx_b, l0_q_norm, l0_k_norm,
              l0_w_out, l0_ffn_norm, l0_ffn_up, l0_ffn_conv_w, l0_ffn_conv_b, l0_ffn_down,
              l1_mix_norm, l1_w_in, l1_conv_w, l1_w_out, l1_ffn_norm, l1_ffn_up, l1_ffn_conv_w,
              l1_ffn_conv_b, l1_ffn_down):
    mixer_params = (
        (l0_mix_norm, l0_w_in, l0_shift_mu, l0_w0, l0_w_lora_up, l0_a0, l0_a_lora_up,
         l0_g_lora_up, l0_k_k, l0_k_a, l0_r_k, l0_lnx_w, l0_lnx_b, l0_q_norm, l0_k_norm,
         l0_w_out),
        (l1_mix_norm, l1_w_in, l1_conv_w, l1_w_out),
    )
    ffn_params = (
        (l0_ffn_norm, l0_ffn_up, l0_ffn_conv_w, l0_ffn_conv_b, l0_ffn_down),
        (l1_ffn_norm, l1_ffn_up, l1_ffn_conv_w, l1_ffn_conv_b, l1_ffn_down),
    )
    for layer in range(DEPTH):
        if layer % 2 == 0:
            x = x + rwkv_moba_mixer(x, rel_bias, *mixer_params[layer])
        else:
            x = x + short_conv_mixer(x, *mixer_params[layer])
        x = x + conv_glu_ffn(x, *ffn_params[layer])
    return x
```

```python
from contextlib import ExitStack
import numpy as np
import concourse.bass as bass
import concourse.mybir as mybir

F32 = mybir.dt.float32
BF16 = mybir.dt.bfloat16
AF = mybir.ActivationFunctionType
ALU = mybir.AluOpType
AX = mybir.AxisListType

COMPUTE = ("pe", "act", "dve", "pool")
EPOCH = 30000
DMA_POOL = 8


class Buf:
    __slots__ = ("name", "lw", "rd_eng", "rd_dma")

    def __init__(self, name=""):
        self.name = name
        self.lw = None
        self.rd_eng = {}
        self.rd_dma = []


class Op:
    __slots__ = ("eng", "fn", "deps", "has_dep", "sig", "clock", "waits", "is_dma", "idx")

    def __init__(self, eng, fn, is_dma):
        self.eng = eng
        self.fn = fn
        self.deps = ()
        self.has_dep = False
        self.sig = None
        self.clock = None
        self.waits = ()
        self.is_dma = is_dma


class Prog:
    def __init__(self, nc):
        self.nc = nc
        self.es = ExitStack()
        self.ops = []
        self.sems = {}
        self.nsem = 0
        self.final_dmas = []
        self.cur_barrier = None
        self.last_eng = {}
        self.dmas_since = []

    def sbuf(self, name, shape, dtype):
        t = self.es.enter_context(self.nc.sbuf_tensor(name, list(shape), dtype))
        return t

    def psum(self, name, shape, dtype):
        return self.es.enter_context(self.nc.psum_tensor(name, list(shape), dtype))

    def sem(self, key):
        if key not in self.sems:
            self.sems[key] = self.es.enter_context(self.nc.semaphore("s%d" % self.nsem))
            self.nsem += 1
        return self.sems[key]

    def _record(self, o, r, w):
        deps = set()
        for b in r:
            if b.lw is not None:
                deps.add(b.lw)
        for b in w:
            if b.lw is not None:
                deps.add(b.lw)
            deps.update(b.rd_eng.values())
            deps.update(b.rd_dma)
        if o.eng == "pe":
            deps = {d for d in deps if d.eng != "pe" or d.is_dma}
        if self.cur_barrier is not None:
            deps.add(self.cur_barrier)
        deps.discard(o)
        o.deps = tuple(deps)
        if o.is_dma:
            self.dmas_since.append(o)
        else:
            self.last_eng[o.eng] = o
        for b in r:
            if o.is_dma:
                b.rd_dma.append(o)
            else:
                b.rd_eng[o.eng] = o
        for b in w:
            b.lw = o
            b.rd_eng = {}
            b.rd_dma = []
        self.ops.append(o)
        return o

    def op(self, eng, fn, r=(), w=()):
        return self._record(Op(eng, fn, False), r, w)

    def dma(self, q, out, in_, r=(), w=(), final=False):
        o = Op(q, (lambda e, out=out, in_=in_: e.dma_start(out=out, in_=in_)), True)
        self._record(o, r, w)
        if final:
            self.final_dmas.append(o)
        return o

    def barrier(self):
        if not hasattr(self, "_dummy"):
            self._dummy = self.sbuf("bar_dummy", [128, 2], F32)
        d = self._dummy
        o = Op("pool", (lambda e: e.memset(d[:, 0:1], 0.0)), False)
        deps = set(self.last_eng.values()) | set(self.dmas_since)
        o.deps = tuple(deps)
        self.ops.append(o)
        self.cur_barrier = o
        self.last_eng = {"pool": o}
        self.dmas_since = []
        return o

    def emit(self):
        nc = self.nc
        ops = self.ops
        for o in ops:
            for d in o.deps:
                d.has_dep = True
        for o in self.final_dmas:
            o.has_dep = True
        engs = ("pe", "act", "dve", "pool", "sp")
        know = {e: {} for e in engs}
        cnt = {e: 0 for e in engs}
        dcnt = {e: 0 for e in engs}
        per_eng = {e: [] for e in engs}
        dma_prev = {}
        for o in ops:
            e = o.eng
            k = know[e]
            need = {}
            deps = list(o.deps)
            if o.is_dma:
                slot = dcnt[e] % DMA_POOL
                prev = dma_prev.get((e, slot))
                if prev is not None:
                    deps.append(prev)
            for d in deps:
                key, val = d.sig
                if k.get(key, 0) < val:
                    if need.get(key, (0, None))[0] < val:
                        need[key] = (val, d)
            waits = []
            for key, (val, d) in need.items():
                waits.append((key, val))
                for kk, vv in d.clock.items():
                    if k.get(kk, 0) < vv:
                        k[kk] = vv
            o.waits = waits
            if o.is_dma:
                j = dcnt[e]
                dcnt[e] += 1
                key = ("dma", e, j % DMA_POOL)
                o.sig = (key, 16 * (j // DMA_POOL + 1))
                dma_prev[(e, j % DMA_POOL)] = o
                o.clock = dict(k)
                o.clock[key] = o.sig[1]
            elif o.has_dep:
                cnt[e] += 1
                ep, v = divmod(cnt[e] - 1, EPOCH)
                key = ("c", e, ep)
                o.sig = (key, v + 1)
                o.clock = dict(k)
                o.clock[key] = v + 1
            per_eng[e].append(o)
        for o in ops:
            if o.sig is not None:
                self.sem(o.sig[0])
        final_waits = [o.sig for o in self.final_dmas]
        nc_eng = {"pe": "tensor", "act": "scalar", "dve": "vector", "pool": "gpsimd", "sp": "sync"}
        with nc.Block() as block:
            def make(e):
                def body(eng):
                    for o in per_eng[e]:
                        for key, val in o.waits:
                            eng.wait_ge(self.sems[key], val)
                        ins = o.fn(eng)
                        if o.sig is not None:
                            ins.then_inc(self.sems[o.sig[0]], 16 if o.is_dma else 1)
                    if e == "sp":
                        for key, val in final_waits:
                            eng.wait_ge(self.sems[key], val)
                return body
            for e in engs:
                if per_eng[e] or e == "sp":
                    getattr(block, nc_eng[e])(make(e))
        n = {e: len(per_eng[e]) for e in engs}
        return n

    def close(self):
        self.es.close()


D = 1024; T = 2048; NB = 2; TOK = NB * T; TT = 512; NT = TOK // TT; DFF = 2816; NH = 22


class Ctx:
    pass


def mk_consts(P, C):
    nc = P.nc
    C.ident = P.sbuf("ident", [128, 128], F32); C.b_ident = Buf()
    P.op("pool", lambda e: e.memset(C.ident[:], 1.0), w=[C.b_ident])
    P.op("pool", lambda e: e.affine_select(C.ident[:], C.ident[:], pattern=[[-1, 128]], compare_op=ALU.is_equal,
                                           fill=0.0, base=0, channel_multiplier=1), r=[C.b_ident], w=[C.b_ident])
    C.ones_bf = P.sbuf("ones_bf", [128, 128], BF16); C.b_ones = Buf()
    P.op("pool", lambda e: e.memset(C.ones_bf[:], 1.0), w=[C.b_ones])
    C.cast_i = 0
    C.colstg = P.sbuf("colstg", [64, 128], F32); C.b_colstg = Buf()


def load_w_bf16(P, C, w_dram, dst, bdst, K, N, stg, bstg):
    kc = K // 128
    wv = w_dram.rearrange("(c p) n -> p c n", p=128)
    CW = stg[0].shape[1]
    for c in range(kc):
        for n0 in range(0, N, CW):
            n1 = min(N, n0 + CW)
            i = C.cast_i % len(stg)
            C.cast_i += 1
            st, bs = stg[i], bstg[i]
            P.dma("sp", st[:, 0:n1 - n0], wv[:, c, n0:n1], w=[bs])
            eng = ("act", "pool", "dve")[C.cast_i % 3] if False else ("act", "pool")[C.cast_i % 2]
            if eng == "act":
                P.op("act", lambda e, st=st, c=c, n0=n0, n1=n1: e.copy(dst[:, c, n0:n1], st[:, 0:n1 - n0]), r=[bs], w=[bdst])
            else:
                P.op(eng, lambda e, st=st, c=c, n0=n0, n1=n1: e.tensor_copy(dst[:, c, n0:n1], st[:, 0:n1 - n0]), r=[bs], w=[bdst])


def load_cols(P, C, v_dram, dst, bdst, n, ps, bps):
    P.dma("pool", C.colstg[0:n, :], v_dram.rearrange("(c p) -> c p", p=128), w=[C.b_colstg])
    P.op("pe", lambda e: e.transpose(ps[:, 0:n], C.colstg[0:n, :], C.ident[0:n, 0:n]), r=[C.b_colstg, C.b_ident], w=[bps])
    P.op("act", lambda e: e.copy(dst, ps[:, 0:n]), r=[bps], w=[bdst])


def rms_norm_tile(P, C, xf, bxf, g, bg, xn, bxn, sq, bsq, ps_ss, bps, rstd, brstd):
    for c in range(8):
        P.op("act", lambda e, c=c: e.activation(sq[:, c, :], xf[:, c, :], AF.Square), r=[bxf], w=[bsq])
    for c in range(8):
        P.op("pe", lambda e, c=c: e.matmul(ps_ss[:], C.ones_bf[:], sq[:, c, :], start=(c == 0), stop=(c == 7)),
             r=[bsq, C.b_ones], w=[bps])
    P.op("act", lambda e: e.activation(rstd[:], ps_ss[:], AF.Sqrt, bias=1e-6, scale=1.0 / D), r=[bps], w=[brstd])
    P.op("dve", lambda e: e.reciprocal(rstd[:], rstd[:]), r=[brstd], w=[brstd])
    for c in range(8):
        P.op("dve", lambda e, c=c: e.scalar_tensor_tensor(xn[:, c, :], xf[:, c, :], g[:, c:c + 1], rstd[:],
                                                          op0=ALU.mult, op1=ALU.mult), r=[bxf, bg, brstd], w=[bxn])


def conv3(P, src_ps, bsrc, zb, bzb, acc, bacc, w3, bw3, ch, bias, hz, bhz, first, eng_copy="act"):
    if first:
        P.op("pool", lambda e: e.memset(zb[:, 0:2], 0.0), w=[bzb])
    else:
        P.op("pool", lambda e: e.tensor_copy(zb[:, 0:2], hz[:, ch, :]), r=[bhz], w=[bzb])
    P.op("act", lambda e: e.copy(zb[:, 2:514], src_ps), r=[bsrc], w=[bzb])
    if bias is not None:
        P.op("act", lambda e: e.activation(acc[:], src_ps, AF.Identity, bias=bias, scale=w3[:, 2, ch:ch + 1]),
             r=[bsrc, bw3], w=[bacc])
    else:
        P.op("act", lambda e: e.activation(acc[:], src_ps, AF.Copy, scale=w3[:, 2, ch:ch + 1]),
             r=[bsrc, bw3], w=[bacc])
    P.op("dve", lambda e: e.scalar_tensor_tensor(acc[:], zb[:, 1:513], w3[:, 1, ch:ch + 1], acc[:],
                                                 op0=ALU.mult, op1=ALU.add), r=[bzb, bw3, bacc], w=[bacc])
    P.op("dve", lambda e: e.scalar_tensor_tensor(acc[:], zb[:, 0:512], w3[:, 0, ch:ch + 1], acc[:],
                                                 op0=ALU.mult, op1=ALU.add), r=[bzb, bw3, bacc], w=[bacc])
    P.op("pool", lambda e: e.tensor_copy(hz[:, ch, :], zb[:, 512:514]), r=[bzb], w=[bhz])


def phase_ffn(P, C, x_in, x_out, out_tok, ws, tag):
    nc = P.nc
    norm_w, up, conv_w, conv_b, down = ws
    S = lambda n, sh, dt: P.sbuf(tag + n, sh, dt)
    wup = S("wup", [128, 8, 2 * DFF], BF16); bwup = Buf()
    wdn = S("wdn", [128, NH, D], BF16); bwdn = Buf()
    stg = [S("stg%d" % i, [128, 512], F32) for i in range(2)]; bstg = [Buf(), Buf()]
    g = S("g", [128, 8], F32); bg = Buf()
    w3 = S("w3", [128, 3, 44], F32); bw3 = Buf()
    cb = S("cb", [128, 44], F32); bcb = Buf()
    load_w_bf16(P, C, up, wup, bwup, D, 2 * DFF, stg, bstg)
    load_w_bf16(P, C, down, wdn, bwdn, DFF, D, stg, bstg)
    xf1 = S("xf", [128, 8, TT], F32); bxf1 = Buf()
    xf = [xf1, xf1]; bxf = [bxf1, bxf1]
    xn = S("xn", [128, 8, TT], BF16); bxn = Buf()
    rstd = S("rstd", [128, TT], F32); brstd = Buf()
    hb = S("hb", [128, NH, TT], BF16); bhb = Buf()
    sq = hb; bsq = bhb
    hz = S("hz", [128, 44, 2], F32); bhz = Buf()
    def one(n, w):
        t_ = S(n, [128, w], F32); b_ = Buf()
        return [t_, t_], [b_, b_]
    za, bza = one("za", 514); zu, bzu = one("zu", 514)
    aa, baa = one("aa", TT); au, bau = one("au", TT); sa, bsa = one("sa", TT)
    xo = [S("xo%d" % i, [128, TT], F32) for i in range(2)]; bxo = [Buf(), Buf()]
    psA = [P.psum(tag + "psA%d" % i, [128, TT], F32) for i in range(2)]; bpsA = [Buf(), Buf()]
    psU = [P.psum(tag + "psU%d" % i, [128, TT], F32) for i in range(2)]; bpsU = [Buf(), Buf()]
    psS = P.psum(tag + "psS", [128, TT], F32); bpsS = Buf()
    psD = [P.psum(tag + "psD%d" % i, [128, TT], F32) for i in range(2)]; bpsD = [Buf(), Buf()]
    load_cols(P, C, norm_w, g[:, 0:8], bg, 8, psS, bpsS)
    for k in range(3):
        load_cols(P, C, conv_w[k], w3[:, k, :], bw3, 44, psS, bpsS)
    load_cols(P, C, conv_b, cb[:, 0:44], bcb, 44, psS, bpsS)
    if out_tok is not None:
        pst = P.psum(tag + "psT", [128, TT], F32); bpst = Buf()
        ob = [S("ob%d" % i, [128, TT], F32) for i in range(2)]; bob = [Buf(), Buf()]
        otv = out_tok.rearrange("(n s p) d -> n p s d", p=128, s=4)
    xin_v = x_in.rearrange("(c p) t -> p c t", p=128)
    xout_v = x_out.rearrange("(c p) t -> p c t", p=128) if x_out is not None else None
    for t in range(NT):
        first = (t % (T // TT) == 0)
        X, bX = xf[t % 2], bxf[t % 2]
        P.dma("pool", X[:], xin_v[:, :, t * TT:(t + 1) * TT], w=[bX])
        rms_norm_tile(P, C, X, bX, g, bg, xn, bxn, sq, bsq, psS, bpsS, rstd, brstd)
        for j in range(NH):
            k = j % 2
            for c in range(8):
                P.op("pe", lambda e, c=c, j=j, k=k: e.matmul(psA[k][:], wup[:, c, j * 128:(j + 1) * 128], xn[:, c, :],
                                                             start=(c == 0), stop=(c == 7)), r=[bwup, bxn], w=[bpsA[k]])
            for c in range(8):
                P.op("pe", lambda e, c=c, j=j, k=k: e.matmul(psU[k][:], wup[:, c, DFF + j * 128:DFF + (j + 1) * 128], xn[:, c, :],
                                                             start=(c == 0), stop=(c == 7)), r=[bwup, bxn], w=[bpsU[k]])
            conv3(P, psA[k][:], bpsA[k], za[k], bza[k], aa[k], baa[k], w3, bw3, j, cb[:, j:j + 1], hz, bhz, first)
            conv3(P, psU[k][:], bpsU[k], zu[k], bzu[k], au[k], bau[k], w3, bw3, NH + j, cb[:, NH + j:NH + j + 1], hz, bhz, first)
            P.op("act", lambda e, k=k: e.activation(sa[k][:], aa[k][:], AF.Silu), r=[baa[k]], w=[bsa[k]])
            P.op("dve", lambda e, k=k, j=j: e.tensor_tensor(hb[:, j, :], sa[k][:], au[k][:], op=ALU.mult),
                 r=[bsa[k], bau[k]], w=[bhb])
        for m in range(8):
            k = m % 2
            XO, bXO = xo[k], bxo[k]
            for j in range(NH):
                P.op("pe", lambda e, m=m, j=j, k=k: e.matmul(psD[k][:], wdn[:, j, m * 128:(m + 1) * 128], hb[:, j, :],
                                                             start=(j == 0), stop=(j == NH - 1)), r=[bwdn, bhb], w=[bpsD[k]])
            P.op("dve", lambda e, m=m, k=k, XO=XO, X=X: e.tensor_tensor(XO[:], psD[k][:], X[:, m, :], op=ALU.add),
                 r=[bpsD[k], bX], w=[bXO])
            if xout_v is not None:
                P.dma("sp", xout_v[:, m, t * TT:(t + 1) * TT], XO[:], r=[bXO])
            if out_tok is not None:
                for s in range(4):
                    P.op("pe", lambda e, s=s, XO=XO: e.transpose(pst[:, s * 128:(s + 1) * 128], XO[:, s * 128:(s + 1) * 128], C.ident[:]),
                         r=[bXO, C.b_ident], w=[bpst])
                P.op("act", lambda e, k=k: e.copy(ob[k][:], pst[:]), r=[bpst], w=[bob[k]])
                P.dma("sp", otv[t, :, :, m * 128:(m + 1) * 128], ob[k][:].rearrange("p (s f) -> p s f", s=4), r=[bob[k]], final=True)


def phase_l1mix(P, C, x_in, x_out, ws, tag):
    norm_w, w_in, conv_w, w_out = ws
    S = lambda n, sh, dt: P.sbuf(tag + n, sh, dt)
    win = S("win", [128, 8, 3 * D], BF16); bwin = Buf()
    wout = S("wout", [128, 8, D], BF16); bwout = Buf()
    stg = [S("stg%d" % i, [128, 2048], F32) for i in range(2)]; bstg = [Buf(), Buf()]
    g = S("g", [128, 8], F32); bg = Buf()
    w3 = S("w3", [128, 3, 8], F32); bw3 = Buf()
    load_w_bf16(P, C, w_in, win, bwin, D, 3 * D, stg, bstg)
    load_w_bf16(P, C, w_out, wout, bwout, D, D, stg, bstg)
    xf = [S("xf%d" % i, [128, 8, TT], F32) for i in range(2)]; bxf = [Buf(), Buf()]
    xn = S("xn", [128, 8, TT], BF16); bxn = Buf()
    sq = S("sq", [128, 8, TT], BF16); bsq = Buf()
    rstd = S("rstd", [128, TT], F32); brstd = Buf()
    gm = S("gm", [128, 8, TT], BF16); bgm = Buf()
    hz = S("hz", [128, 8, 2], F32); bhz = Buf()
    zc = [S("zc%d" % i, [128, 514], F32) for i in range(2)]; bzc = [Buf(), Buf()]
    cs = [S("cs%d" % i, [128, TT], F32) for i in range(2)]; bcs = [Buf(), Buf()]
    chh = [S("ch%d" % i, [128, TT], F32) for i in range(2)]; bchh = [Buf(), Buf()]
    ac = [S("ac%d" % i, [128, TT], F32) for i in range(2)]; bac = [Buf(), Buf()]
    xo = [S("xo%d" % i, [128, 8, TT], F32) for i in range(2)]; bxo = [Buf(), Buf()]
    psB = [P.psum(tag + "psB%d" % i, [128, TT], F32) for i in range(2)]; bpsB = [Buf(), Buf()]
    psC = [P.psum(tag + "psC%d" % i, [128, TT], F32) for i in range(2)]; bpsC = [Buf(), Buf()]
    psH = [P.psum(tag + "psH%d" % i, [128, TT], F32) for i in range(2)]; bpsH = [Buf(), Buf()]
    psS = P.psum(tag + "psS", [128, TT], F32); bpsS = Buf()
    psD = P.psum(tag + "psD", [128, TT], F32); bpsD = Buf()
    load_cols(P, C, norm_w, g[:, 0:8], bg, 8, psS, bpsS)
    for k in range(3):
        load_cols(P, C, conv_w[k], w3[:, k, :], bw3, 8, psS, bpsS)
    xin_v = x_in.rearrange("(c p) t -> p c t", p=128)
    xout_v = x_out.rearrange("(c p) t -> p c t", p=128)
    for t in range(NT):
        first = (t % (T // TT) == 0)
        X, bX = xf[t % 2], bxf[t % 2]
        P.dma("pool", X[:], xin_v[:, :, t * TT:(t + 1) * TT], w=[bX])
        rms_norm_tile(P, C, X, bX, g, bg, xn, bxn, sq, bsq, psS, bpsS, rstd, brstd)
        for j in range(8):
            k = j % 2
            for (ps, bps, off) in ((psB[k], bpsB[k], 0), (psC[k], bpsC[k], D), (psH[k], bpsH[k], 2 * D)):
                for c in range(8):
                    P.op("pe", lambda e, c=c, j=j, ps=ps, off=off: e.matmul(ps[:], win[:, c, off + j * 128:off + (j + 1) * 128],
                                                                            xn[:, c, :], start=(c == 0), stop=(c == 7)),
                         r=[bwin, bxn], w=[bps])
            P.op("act", lambda e, k=k: e.copy(cs[k][:], psC[k][:]), r=[bpsC[k]], w=[bcs[k]])
            P.op("dve", lambda e, k=k: e.tensor_tensor(chh[k][:], cs[k][:], psH[k][:], op=ALU.mult),
                 r=[bcs[k], bpsH[k]], w=[bchh[k]])
            conv3(P, chh[k][:], bchh[k], zc[k], bzc[k], ac[k], bac[k], w3, bw3, j, None, hz, bhz, first)
            P.op("dve", lambda e, k=k, j=j: e.tensor_tensor(gm[:, j, :], ac[k][:], psB[k][:], op=ALU.mult),
                 r=[bac[k], bpsB[k]], w=[bgm])
        XO, bXO = xo[t % 2], bxo[t % 2]
        for m in range(8):
            for j in range(8):
                P.op("pe", lambda e, m=m, j=j: e.matmul(psD[:], wout[:, j, m * 128:(m + 1) * 128], gm[:, j, :],
                                                        start=(j == 0), stop=(j == 7)), r=[bwout, bgm], w=[bpsD])
            P.op("dve", lambda e, m=m, XO=XO, X=X: e.tensor_tensor(XO[:, m, :], psD[:], X[:, m, :], op=ALU.add),
                 r=[bpsD, bX], w=[bXO])
        P.dma("sp", xout_v[:, :, t * TT:(t + 1) * TT], XO[:], r=[bXO])


RW_COLS = 1792
EW = 2816
NI = EW + 127
CDEC = 0.6065306597126334


def bcast(ap, pos, count):
    l = [list(x) for x in ap.ap]
    l.insert(pos, [0, count])
    return bass.AP(ap.tensor, ap.offset, l)


def run_interleaved(gens):
    gens = list(gens)
    while gens:
        for g_ in list(gens):
            try:
                next(g_)
            except StopIteration:
                gens.remove(g_)


def phase_l0(P, C, x_tok, x_out, W, t5oh, wb_dram, ev_dram, tz_dram, tag, stop=None):
    nc = P.nc
    S = lambda n, sh, dt: P.sbuf(tag + n, sh, dt)
    def SB(n, sh, dt):
        return S(n, sh, dt), Buf()
    PB = [P.psum(tag + "pb%d" % i, [128, 512], F32) for i in range(8)]; bPB = [Buf() for _ in range(8)]
    xf, bxf = SB("xf", [128, 8, TT], F32)
    stg = [xf[:, 0:2, :].rearrange("p a b -> p (a b)"), xf[:, 2:4, :].rearrange("p a b -> p (a b)")]
    stgb = [xf[:, 4:5, :].rearrange("p a b -> p (a b)").bitcast(BF16), xf[:, 5:6, :].rearrange("p a b -> p (a b)").bitcast(BF16)]
    bstg = [Buf(), Buf()]; bstgb = [Buf(), Buf()]
    wv = W["l0_w_in"].rearrange("(c p) n -> p c n", p=128)
    i_ = 0
    for c in range(8):
        for n0 in range(0, 3328, 1024):
            n1 = min(3328, n0 + 1024); k = i_ % 2; i_ += 1
            P.dma("sp", stg[k][:, 0:n1 - n0], wv[:, c, n0:n1], w=[bstg[k]])
            P.op("act" if k == 0 else "pool",
                 (lambda e, k=k, n=n1 - n0: e.copy(stgb[k][:, 0:n], stg[k][:, 0:n])) if k == 0 else
                 (lambda e, k=k, n=n1 - n0: e.tensor_copy(stgb[k][:, 0:n], stg[k][:, 0:n])), r=[bstg[k]], w=[bstgb[k]])
            P.dma("pool", wb_dram[:, c, n0:n1], stgb[k][:, 0:n1 - n0], r=[bstgb[k]])
    wout, bwout = SB("wout", [64, 16, D], BF16)
    wo_v = W["l0_w_out"].rearrange("(h p) n -> p h n", p=64)
    for h in range(16):
        k = h % 2
        P.dma("sp", stg[k][0:64, 0:1024], wo_v[:, h, :], w=[bstg[k]])
        P.op("act", lambda e, k=k, h=h: e.copy(wout[:, h, :], stg[k][0:64, 0:1024]), r=[bstg[k]], w=[bwout])
    lwa, blwa = SB("lwa", [128, 512], BF16)
    glo, bglo = SB("glo", [128, 512], BF16)
    P.dma("sp", stg[0][0:64, 0:512], W["l0_w_lora_up"], w=[bstg[0]])
    P.dma("sp", stg[0][64:128, 0:512], W["l0_a_lora_up"], w=[bstg[0]])
    P.op("act", lambda e: e.copy(lwa[:], stg[0][:, 0:512]), r=[bstg[0]], w=[blwa])
    P.dma("sp", stg[1][:, 0:512], W["l0_g_lora_up"], w=[bstg[1]])
    P.op("act", lambda e: e.copy(glo[:], stg[1][:, 0:512]), r=[bstg[1]], w=[bglo])
    prm, bprm = SB("prm", [128, 64], F32)
    g_ = prm[:, 0:8]; mu = prm[:, 8:22]; w0 = prm[:, 22:26]; a0 = prm[:, 26:30]; kk_ = prm[:, 30:34]
    ka_ = prm[:, 34:38]; rk_ = prm[:, 38:42]; qn = prm[:, 42:43]; kn = prm[:, 43:44]
    lnw = prm[0:64, 44:52]; lnb = prm[0:64, 52:60]
    load_cols(P, C, W["l0_mix_norm"], g_, bprm, 8, PB[0], bPB[0])
    load_cols(P, C, W["l0_shift_mu"], mu, bprm, 14, PB[0], bPB[0])
    load_cols(P, C, W["l0_w0"], w0, bprm, 4, PB[0], bPB[0])
    load_cols(P, C, W["l0_a0"], a0, bprm, 4, PB[0], bPB[0])
    load_cols(P, C, W["l0_k_k"], kk_, bprm, 4, PB[0], bPB[0])
    load_cols(P, C, W["l0_k_a"], ka_, bprm, 4, PB[0], bPB[0])
    load_cols(P, C, W["l0_r_k"].rearrange("h d -> (h d)"), rk_, bprm, 4, PB[0], bPB[0])
    for (src, dst) in ((W["l0_q_norm"], qn), (W["l0_k_norm"], kn)):
        P.dma("pool", C.colstg[0:1, 0:64], src.rearrange("(o p) -> o p", o=1), w=[C.b_colstg])
        P.dma("pool", C.colstg[0:1, 64:128], src.rearrange("(o p) -> o p", o=1), w=[C.b_colstg])
        P.op("pe", lambda e: e.transpose(PB[0][:, 0:1], C.colstg[0:1, :], C.ident[0:1, 0:1]), r=[C.b_colstg, C.b_ident], w=[bPB[0]])
        P.op("act", lambda e, dst=dst: e.copy(dst, PB[0][:, 0:1]), r=[bPB[0]], w=[bprm])
    for (src, dst) in ((W["l0_lnx_w"], lnw), (W["l0_lnx_b"], lnb)):
        P.dma("pool", C.colstg[0:8, 0:64], src.rearrange("(h p) -> h p", p=64), w=[C.b_colstg])
        P.op("pe", lambda e: e.transpose(PB[0][0:64, 0:8], C.colstg[0:8, 0:64], C.ident[0:8, 0:8]), r=[C.b_colstg, C.b_ident], w=[bPB[0]])
        P.op("act", lambda e, dst=dst: e.copy(dst, PB[0][0:64, 0:8]), r=[bPB[0]], w=[bprm])
    if stop == "A":
        return
    cst, bcst = SB("cstb", [128, 3, 128], BF16)
    identb = cst[:, 0, :]; Jb = cst[:, 1, :]; BD = cst[:, 2, :]
    cf, bcf = SB("cstf", [128, 128], F32)
    P.op("pool", lambda e: e.memset(cf[:], 1.0), w=[bcf])
    P.op("pool", lambda e: e.affine_select(cf[:], cf[:], pattern=[[1, 128]], compare_op=ALU.is_equal, fill=0.0, base=-127,
                                           channel_multiplier=1), r=[bcf], w=[bcf])
    P.op("pool", lambda e: e.tensor_copy(Jb, cf[:]), r=[bcf], w=[bcst])
    P.op("pool", lambda e: e.tensor_copy(identb, C.ident[:]), r=[C.b_ident], w=[bcst])
    P.op("pool", lambda e: e.memset(BD, 0.0), w=[bcst])
    P.op("pool", lambda e: e.memset(cst[0:64, 2, 0:64], 1.0), r=[bcst], w=[bcst])
    P.op("pool", lambda e: e.memset(cst[64:128, 2, 64:128], 1.0), r=[bcst], w=[bcst])
    ones64f, bo64 = SB("ones64f", [64, 64], F32)
    P.op("pool", lambda e: e.memset(ones64f[:], 1.0), w=[bo64])
    MS, bMS = SB("MS", [128, 512], F32)
    P.op("pool", lambda e: e.memset(MS[:], 1.0), w=[bMS])
    P.op("pool", lambda e: e.memset(MS[:].rearrange("p (c t) -> p c t", t=64)[:, :, 0:1], 0.0), r=[bMS], w=[bMS])
    mk, bmk = SB("mk", [64, 3, 2, 64], F32)
    P.op("pool", lambda e: e.memset(mk[:], 1.0), w=[bmk])
    for hh in range(2):
        P.op("pool", lambda e, hh=hh: e.affine_select(mk[:, 0, hh, :], mk[:, 0, hh, :], pattern=[[1, 64]], compare_op=ALU.is_ge,
                                                      fill=0.0, base=-1, channel_multiplier=-1), r=[bmk], w=[bmk])
        P.op("pool", lambda e, hh=hh: e.affine_select(mk[:, 1, hh, :], mk[:, 1, hh, :], pattern=[[1, 64]], compare_op=ALU.is_ge,
                                                      fill=0.0, base=0, channel_multiplier=-1), r=[bmk], w=[bmk])
        P.op("pool", lambda e, hh=hh: e.affine_select(mk[:, 2, hh, :], mk[:, 2, hh, :], pattern=[[-1, 64]], compare_op=ALU.is_ge,
                                                      fill=0.0, base=-1, channel_multiplier=1), r=[bmk], w=[bmk])
    MSU = mk[:, 0, :, :]; MU = mk[:, 1, :, :]; MSL = mk[:, 2, :, :]
    pm, bpm = SB("pm", [128, 3, 8, 8], F32)
    P.op("pool", lambda e: e.memset(pm[:], 0.0), w=[bpm])
    for n in range(8):
        if n > 0:
            P.op("pool", lambda e, n=n: e.memset(pm[:, 0, n, 0:n], 1.0), r=[bpm], w=[bpm])
        P.op("pool", lambda e, n=n: e.memset(pm[:, 1, n, n:n + 1], 1.0), r=[bpm], w=[bpm])
    P.op("pool", lambda e: e.tensor_scalar(pm[:, 2, :, :], pm[:, 0, :, :], -1.0, 1e30, op0=ALU.add, op1=ALU.mult), r=[bpm], w=[bpm])
    if stop == "A2":
        return
    relb, brelb = SB("relb", [33, 8], F32)
    P.op("pool", lambda e: e.memset(relb[32:33, :], -30000.0), w=[brelb])
    P.dma("pool", relb[0:32, :], W["rel_bias"], r=[brelb], w=[brelb])
    oh, boh = SB("oh", [33, 512], F32)
    evb, bevb = SB("evb", [8, 512], BF16)
    for n0 in range(0, NI, 512):
        n1 = min(NI, n0 + 512)
        P.dma("pool", oh[:, 0:n1 - n0], t5oh[:, n0:n1], w=[boh])
        P.op("pe", lambda e, n=n1 - n0: e.matmul(PB[1][0:8, 0:n], relb[:], oh[:, 0:n], start=True, stop=True), r=[brelb, boh], w=[bPB[1]])
        P.op("act", lambda e, n=n1 - n0: e.activation(evb[:, 0:n], PB[1][0:8, 0:n], AF.Exp), r=[bPB[1]], w=[bevb])
        P.dma("pool", ev_dram[:, n0:n1], evb[:, 0:n1 - n0], r=[bevb])
    P.barrier()
    tzb = [S("tz%d" % i, [128, EW], BF16) for i in range(2)]; btz = [Buf(), Buf()]
    for h in range(8):
        hk = tzb[0]
        src = bass.AP(ev_dram.tensor, h * NI, [[1, 128], [1, EW]])
        P.dma("pool", hk[:], src, w=[btz[0]])
        for n0 in range(0, EW, 512):
            n1 = min(EW, n0 + 512)
            P.op("pe", lambda e, n0=n0, n1=n1: e.matmul(PB[1][:, 0:n1 - n0], Jb, tzb[0][:, n0:n1], start=True, stop=True),
                 r=[bcst, btz[0]], w=[bPB[1]])
            P.op("act", lambda e, n0=n0, n1=n1: e.copy(tzb[1][:, n0:n1], PB[1][:, 0:n1 - n0]), r=[bPB[1]], w=[btz[1]])
        P.dma("pool", tz_dram[h], tzb[1][:], r=[btz[1]])
    P.barrier()
    if stop == "B":
        return
    KT, bKT = SB("KT", [128, 4, T], BF16)
    VC, bVC = SB("VC", [128, 16, 512], BF16)
    kmT, bkmT = SB("kmT", [128, 4, 8], BF16)
    kmf, bkmf = SB("kmf", [128, 4, 8], F32)
    P.op("pool", lambda e: e.memset(kmf[:], 0.0), w=[bkmf])
    HP, bHP = SB("HP", [128, 14], F32)
    ST = [S("ST%d" % j, [128, 128], F32) for j in range(4)]; bST = [Buf() for _ in range(4)]
    STb = [S("STb%d" % j, [128, 128], BF16) for j in range(4)]; bSTb = [Buf() for _ in range(4)]
    xn, bxn = SB("xn", [128, 8, TT], BF16)
    yall, byall = SB("yall", [64, 16, TT], BF16)
    rstd, brstd = SB("rstd", [128, TT], F32)
    xt1 = S("xt", [128, D], F32); bxt1 = Buf()
    xt = [xt1, xt1]; bxt = [bxt1, bxt1]
    wr = [S("wr%d" % i, [128, 8, 128], BF16) for i in range(2)]; bwr = [Buf() for _ in range(2)]
    wvv, bwvv = SB("wvv", [128, 8, 512], BF16)
    sq = wvv; bsq = bwvv
    QZ, bQZ = SB("QZ", [128, 4, 2, TT], BF16)
    NMT, bNMT = SB("NMT", [64, TT], BF16)
    tda, btda = SB("tdaZ", [128, 2, TT], BF16)
    sdg, bsdg = SB("sdg", [128, TT], BF16)
    praw, bpraw = SB("praw", [128, 513], F32)
    dif, bdif = SB("dif", [128, TT], F32)
    F = {}
    for n in ("r", "k", "v", "sg", "a", "lcs", "t1", "e2", "e3", "kp"):
        F[n] = SB("f_" + n, [128, TT], F32)
    F["e1"] = F["t1"]; F["kk0"] = F["sg"]; F["rn"] = (dif, bdif)
    AR, bAR = SB("ARZ", [128, 2, 2, TT], BF16)
    BKV, bBKV = SB("BKV", [128, 3, TT], BF16)
    rkp, brkp = SB("rkpZ", [128, 2, TT], BF16)
    for (zt, bz_) in ((QZ, bQZ), (tda, btda), (AR, bAR), (rkp, brkp)):
        P.op("pool", lambda e, zt=zt: e.memset(zt[:], 0.0), w=[bz_])
    sqk, bsqk = SB("sqk", [128, TT], BF16)
    GC, bGC = SB("GC", [128, 8], F32)
    Wl = [S("Wl%d" % i, [64, 2, 3, 64], BF16) for i in range(2)]; bWl = [Buf(), Buf()]
    AM, bAM = SB("AM", [64, 2, 3, 64], BF16)
    BKVt, bBKVt = SB("BKVt", [64, 3, 128], BF16)
    Xb, bXb = SB("Xb", [64, 2, 64], BF16)
    Ub, bUb = SB("Ub", [64, 2, 64], BF16)
    YT, bYT = SB("YT", [64, 2, TT], F32)
    tS, btS = SB("tS", [128, 128], F32)
    G = {}
    for n, fn_ in (("mean", "sg"), ("msq", "lcs"), ("var", "t1"), ("cen", "e2"), ("ysq", "e3"), ("vs", "kp"), ("bon", "a"), ("rden", "rn")):
        G[n] = (F[fn_][0][0:64, :], F[fn_][1])
    PTb = [S("PT%d" % i, [128, TT], BF16) for i in range(2)]; bPTb = [Buf(), Buf()]
    gsel, bgsel = SB("gsel", [128, 8, 8], F32)
    gcmp = dif[:].rearrange("p (a b c) -> p a b c", a=8, b=8); bgcmp = bdif
    grk, bgrk = SB("grk", [128, 8, 8], F32)
    xs = [praw[:, 0:512], F["kp"][0][:, :]]; bxs = [bpraw, F["kp"][1]]
    xout_v = x_out.rearrange("(c p) t -> p c t", p=128)
    wcnt = [0]

    def wchunk(ch):
        k = wcnt[0] % 2; wcnt[0] += 1
        P.dma("sp", wr[k][:], wb_dram[:, :, ch * 128:(ch + 1) * 128], w=[bwr[k]])
        return wr[k], bwr[k]

    def proj(ch, ps, bps):
        wt, bw = wchunk(ch)
        for c in range(8):
            P.op("pe", lambda e, c=c, wt=wt: e.matmul(ps[:], wt[:, c, :], xn[:, c, :], start=(c == 0), stop=(c == 7)),
                 r=[bw, bxn], w=[bps])

    def shift_mix(ch, ps, bps, dst, bdst, first):
        if first:
            P.op("pool", lambda e: e.memset(praw[:, 0:1], 0.0), w=[bpraw])
        else:
            P.op("pool", lambda e: e.tensor_copy(praw[:, 0:1], HP[:, ch:ch + 1]), r=[bHP], w=[bpraw])
        P.op("act", lambda e: e.copy(praw[:, 1:513], ps[:]), r=[bps], w=[bpraw])
        P.op("dve", lambda e: e.tensor_tensor(dif[:], praw[:, 0:512], praw[:, 1:513], op=ALU.subtract), r=[bpraw], w=[bdif])
        P.op("dve", lambda e: e.scalar_tensor_tensor(dst, dif[:], mu[:, ch:ch + 1], praw[:, 1:513], op0=ALU.mult, op1=ALU.add),
             r=[bdif, bpraw, bprm], w=[bdst])
        P.op("pool", lambda e: e.tensor_copy(HP[:, ch:ch + 1], praw[:, 512:513]), r=[bpraw], w=[bHP])

    for t in range(NT):
        b = t // 4; i = t % 4; first = (i == 0)
        for s in range(4):
            k = s % 2
            r0 = t * TT + s * 128
            P.dma("pool", xt[k][:], x_tok[r0:r0 + 128, :], w=[bxt[k]])
            for half in range(2):
                pb = 2 + half
                for cc in range(4):
                    c = half * 4 + cc
                    P.op("pe", lambda e, c=c, cc=cc, k=k, pb=pb: e.transpose(PB[pb][:, cc * 128:(cc + 1) * 128],
                                                                            xt[k][:, c * 128:(c + 1) * 128], C.ident[:]),
                         r=[bxt[k], C.b_ident], w=[bPB[pb]])
                P.op("act" if half == 0 else "dve",
                     (lambda e, s=s, pb=pb, half=half: e.copy(xf[:, half * 4:half * 4 + 4, s * 128:(s + 1) * 128],
                                                              PB[pb][:].rearrange("p (c f) -> p c f", c=4))) if half == 0 else
                     (lambda e, s=s, pb=pb, half=half: e.tensor_copy(xf[:, half * 4:half * 4 + 4, s * 128:(s + 1) * 128],
                                                                     PB[pb][:].rearrange("p (c f) -> p c f", c=4))),
                     r=[bPB[pb]], w=[bxf])
        rms_norm_tile(P, C, xf, bxf, prm, bprm, xn, bxn, sq, bsq, PB[0], bPB[0], rstd, brstd)
        proj(12, PB[1], bPB[1])
        shift_mix(12, PB[1], bPB[1], dif[:], bdif, first)
        P.op("act", lambda e: e.activation(tda[0:64, 0, :], dif[0:64, :], AF.Tanh), r=[bdif], w=[btda])
        P.op("act", lambda e: e.copy(tda[64:128, 1, :], dif[64:128, :]), r=[bdif], w=[btda])
        proj(13, PB[2], bPB[2])
        shift_mix(13, PB[2], bPB[2], dif[:], bdif, first)
        P.op("act", lambda e: e.activation(sdg[:], dif[:], AF.Sigmoid), r=[bdif], w=[bsdg])
        if stop == "C":
            return
        for pj in range(4):
            for (ch, gain, isq) in ((14 + pj, qn, True), (18 + pj, kn, False)):
                proj(ch, PB[1], bPB[1])
                P.op("act", lambda e: e.activation(sqk[:], PB[1][:], AF.Square), r=[bPB[1]], w=[bsqk])
                P.op("pe", lambda e: e.matmul(PB[2][:], BD, sqk[:], start=True, stop=True), r=[bcst, bsqk], w=[bPB[2]])
                rn_, brn_ = F["rn"]
                P.op("act", lambda e, rn_=rn_: e.activation(rn_[:], PB[2][:], AF.Sqrt, bias=1e-6, scale=1.0 / 64), r=[bPB[2]], w=[brn_])
                P.op("dve", lambda e, rn_=rn_: e.reciprocal(rn_[:], rn_[:]), r=[brn_], w=[brn_])
                if isq:
                    for hh in range(2):
                        hs = slice(64 * hh, 64 * hh + 64)
                        P.op("dve", lambda e, pj=pj, gain=gain, rn_=rn_, hh=hh, hs=hs: e.scalar_tensor_tensor(
                            QZ[hs, pj, hh, :], PB[1][hs, :], gain[hs, :], rn_[hs, :], op0=ALU.mult, op1=ALU.mult),
                             r=[bPB[1], brn_, bprm], w=[bQZ])
                else:
                    P.op("dve", lambda e, pj=pj, gain=gain, rn_=rn_, i=i: e.scalar_tensor_tensor(KT[:, pj, i * TT:(i + 1) * TT], PB[1][:], gain,
                                                                                               rn_[:], op0=ALU.mult, op1=ALU.mult),
                         r=[bPB[1], brn_, bprm], w=[bKT])
                    P.op("dve", lambda e, pj=pj, i=i: e.tensor_reduce(kmf[:, pj, 2 * i:2 * i + 2],
                                                                      KT[:, pj, i * TT:(i + 1) * TT].rearrange("p (b t) -> p b t", b=2),
                                                                      axis=AX.X, op=ALU.add), r=[bKT], w=[bkmf])
        P.op("act", lambda e: e.copy(kmT[:], kmf[:]), r=[bkmf], w=[bkmT])
        if stop == "C2":
            return
        P.dma("sp", wvv[:], wb_dram[:, :, 2816:3328], w=[bwvv])
        for s in range(4):
            pb = 1 + (s % 2)
            for c in range(8):
                P.op("pe", lambda e, c=c, s=s, pb=pb: e.matmul(PB[pb][:], xn[:, c, s * 128:(s + 1) * 128], wvv[:, c, :],
                                                               start=(c == 0), stop=(c == 7)), r=[bxn, bwvv], w=[bPB[pb]])
            P.op("act", lambda e, s=s, pb=pb, i=i: e.copy(VC[:, i * 4 + s, :], PB[pb][:]), r=[bPB[pb]], w=[bVC])
        if stop == "C3":
            return
        for s in range(4):
            n = 2 * i + s // 2
            for h in range(8):
                P.op("pe", lambda e, h=h, s=s: e.matmul(PB[1][:, h * 64:h * 64 + 8], QZ[:, h // 2, h % 2, s * 128:(s + 1) * 128],
                                                       kmT[:, h // 2, :], start=True, stop=True),
                     r=[bQZ, bkmT], w=[bPB[1]])
            if stop == "D1":
                return
            past = bcast(pm[:, 0, n, :], 1, 8); own = bcast(pm[:, 1, n, :], 1, 8); negm = bcast(pm[:, 2, n, :], 1, 8)
            P.op("dve", lambda e, past=past: e.tensor_tensor(gsel[:], PB[1][:].rearrange("p (h j) -> p h j", h=8)[:, :, 0:8], past, op=ALU.mult),
                 r=[bPB[1], bpm], w=[bgsel])
            P.op("dve", lambda e, negm=negm: e.tensor_tensor(gsel[:], gsel[:], negm, op=ALU.add), r=[bgsel, bpm], w=[bgsel])
            P.op("dve", lambda e: e.tensor_tensor(gcmp, bcast(gsel[:], 2, 8), bcast(gsel[:], 3, 8), op=ALU.is_gt),
                 r=[bgsel], w=[bgcmp])
            P.op("dve", lambda e: e.tensor_reduce(grk[:], gcmp, axis=AX.X, op=ALU.add), r=[bgcmp], w=[bgrk])
            P.op("dve", lambda e: e.tensor_single_scalar(grk[:], grk[:], 3.0, op=ALU.is_lt), r=[bgrk], w=[bgrk])
            P.op("dve", lambda e, past=past: e.tensor_tensor(grk[:], grk[:], past, op=ALU.mult), r=[bgrk, bpm], w=[bgrk])
            P.op("dve", lambda e, own=own: e.tensor_tensor(grk[:], grk[:], own, op=ALU.add), r=[bgrk, bpm], w=[bgrk])
            P.op("dve", lambda e: e.tensor_scalar(grk[:], grk[:], -1.0, 30000.0, op0=ALU.add, op1=ALU.mult), r=[bgrk], w=[bgrk])
            if stop == "D2":
                return
            P.op("pe", lambda e: e.transpose(PB[2][0:64, 0:128], grk[:].rearrange("p h j -> p (h j)"), C.ident[:]),
                 r=[bgrk, C.b_ident], w=[bPB[2]])
            P.op("act", lambda e, s=s: e.copy(NMT[:, s * 128:(s + 1) * 128], PB[2][0:64, 0:128]), r=[bPB[2]], w=[bNMT])

        if stop == "D":
            return
        def rw_pair(j):
            r_, br = F["r"]; k_, bk = F["k"]; v_, bv = F["v"]; sg, bsg = F["sg"]; a_, ba = F["a"]; lcs, blcs = F["lcs"]
            t1, bt1 = F["t1"]; e1, be1 = F["e1"]; e2, be2 = F["e2"]; e3, be3 = F["e3"]; kk0, bkk0 = F["kk0"]; rn_, brn_ = F["rn"]
            kp, bkp = F["kp"]
            for (ch, dst, bd) in ((j, r_, br), (4 + j, k_, bk), (8 + j, v_, bv)):
                proj(ch, PB[1], bPB[1])
                shift_mix(ch, PB[1], bPB[1], dst[:], bd, first)
                yield
            P.op("pe", lambda e: e.matmul(PB[1][:], lwa[:, j * 128:(j + 1) * 128], tda[:, 0, :], start=True, stop=True),
                 r=[blwa, btda], w=[bPB[1]])
            P.op("act", lambda e: e.activation(sg[:], PB[1][:], AF.Sigmoid, bias=w0[:, j:j + 1]), r=[bPB[1], bprm], w=[bsg])
            P.op("pe", lambda e: e.matmul(PB[2][:], lwa[:, j * 128:(j + 1) * 128], tda[:, 1, :], start=True, stop=True),
                 r=[blwa, btda], w=[bPB[2]])
            P.op("act", lambda e: e.activation(a_[:], PB[2][:], AF.Sigmoid, bias=a0[:, j:j + 1]), r=[bPB[2], bprm], w=[ba])
            P.op("dve", lambda e: e.tensor_tensor_scan(lcs[:], MS[:], sg[:], 0.0, op0=ALU.mult, op1=ALU.add), r=[bMS, bsg], w=[blcs])
            P.op("pool", lambda e: e.tensor_tensor(t1[:], lcs[:], sg[:], op=ALU.subtract), r=[blcs, bsg], w=[bt1])
            P.op("act", lambda e: e.activation(e1[:], t1[:], AF.Exp, scale=-CDEC), r=[bt1], w=[be1])
            P.op("act", lambda e: e.activation(e2[:], lcs[:], AF.Exp, scale=-CDEC), r=[blcs], w=[be2])
            P.op("act", lambda e: e.activation(e3[:], lcs[:], AF.Exp, scale=CDEC), r=[blcs], w=[be3])
            P.op("pool", lambda e: e.tensor_copy(GC[:], e2[:].rearrange("p (c t) -> p c t", t=64)[:, :, 63]), r=[be2], w=[bGC])
            yield
            P.op("dve", lambda e: e.tensor_scalar(kk0[:], k_[:], kk_[:, j:j + 1], None, op0=ALU.mult), r=[bk, bprm], w=[bkk0])
            P.op("act", lambda e: e.activation(sqk[:], kk0[:], AF.Square), r=[bkk0], w=[bsqk])
            P.op("pe", lambda e: e.matmul(PB[1][:], BD, sqk[:], start=True, stop=True), r=[bcst, bsqk], w=[bPB[1]])
            P.op("act", lambda e: e.activation(rn_[:], PB[1][:], AF.Sqrt, bias=1e-24), r=[bPB[1]], w=[brn_])
            P.op("dve", lambda e: e.reciprocal(rn_[:], rn_[:]), r=[brn_], w=[brn_])
            P.op("dve", lambda e: e.tensor_tensor(kk0[:], kk0[:], rn_[:], op=ALU.mult), r=[bkk0, brn_], w=[bkk0])
            P.op("dve", lambda e: e.tensor_scalar(kp[:], a_[:], -1.0, ka_[:, j:j + 1], op0=ALU.add, op1=ALU.mult), r=[ba, bprm], w=[bkp])
            P.op("dve", lambda e: e.scalar_tensor_tensor(kp[:], kp[:], 1.0, k_[:], op0=ALU.add, op1=ALU.mult), r=[bkp, bk], w=[bkp])
            for hh in range(2):
                hs = slice(64 * hh, 64 * hh + 64)
                P.op("dve", lambda e, hh=hh, hs=hs: e.scalar_tensor_tensor(AR[hs, hh, 0, :], kk0[hs, :], -1.0, e1[hs, :], op0=ALU.mult, op1=ALU.mult),
                     r=[bkk0, be1], w=[bAR])
                P.op("dve", lambda e, hh=hh, hs=hs: e.tensor_tensor(AR[hs, hh, 1, :], r_[hs, :], e2[hs, :], op=ALU.mult), r=[br, be2], w=[bAR])
            P.op("dve", lambda e: e.tensor_tensor(t1[:], kk0[:], a_[:], op=ALU.mult), r=[bkk0, ba], w=[bt1])
            P.op("dve", lambda e: e.tensor_tensor(BKV[:, 0, :], t1[:], e3[:], op=ALU.mult), r=[bt1, be3], w=[bBKV])
            P.op("dve", lambda e: e.tensor_tensor(BKV[:, 1, :], kp[:], e3[:], op=ALU.mult), r=[bkp, be3], w=[bBKV])
            P.op("act", lambda e: e.copy(BKV[:, 2, :], v_[:]), r=[bv], w=[bBKV])
            for hh in range(2):
                hs = slice(64 * hh, 64 * hh + 64)
                P.op("dve", lambda e, hh=hh, hs=hs: e.scalar_tensor_tensor(rkp[hs, hh, :], r_[hs, :], rk_[hs, j:j + 1], kp[hs, :], op0=ALU.mult, op1=ALU.mult),
                     r=[br, bkp, bprm], w=[brkp])
            if first:
                P.op("pool", lambda e: e.memset(ST[j][:], 0.0), w=[bST[j]])
                P.op("pool", lambda e: e.memset(STb[j][:], 0.0), w=[bSTb[j]])
            yield
            pa = PB[3][:].rearrange("p (k h c) -> p k h c", k=2, h=2)
            pn = PB[4][:, 0:128].rearrange("p (h c) -> p h c", h=2)
            pi = PB[5][:].rearrange("p (h c) -> p h c", h=2)
            ptb = PB[6][:].bitcast(BF16)
            import os as _os
            for c in range(int(_os.environ.get('NCH', '8'))):
                cs = slice(c * 64, (c + 1) * 64)
                for hh in range(2):
                    hs = slice(64 * hh, 64 * hh + 64)
                    P.op("pe", lambda e, hh=hh, hs=hs, cs=cs: e.matmul(pa[0:64, 0, hh, :], BKV[:, 0, cs], AR[:, hh, :, cs], start=True, stop=True),
                         r=[bBKV, bAR], w=[bPB[3]])
                    P.op("pe", lambda e, hh=hh, hs=hs, cs=cs: e.matmul(pa[0:64, 1, hh, :], BKV[:, 1, cs], AR[:, hh, :, cs], start=True, stop=True),
                         r=[bBKV, bAR], w=[bPB[3]])
                    P.op("pe", lambda e, hh=hh, hs=hs, cs=cs: e.matmul(pn[0:64, hh, 0:64], AR[:, hh, 0, cs], BKV[:, 0, cs], start=True, stop=True),
                         r=[bBKV, bAR], w=[bPB[4]])
                W0 = Wl[0]
                P.op("dve", lambda e, W0=W0: e.tensor_tensor(W0[:, :, 0, :], pa[0:64, 0, :, 0:64], MSU, op=ALU.mult), r=[bPB[3], bmk], w=[bWl[0]])
                P.op("dve", lambda e, W0=W0: e.tensor_tensor(W0[:, :, 2, :], pn[0:64, :, 0:64], MSL, op=ALU.mult), r=[bPB[4], bmk], w=[bWl[0]])
                P.op("pool", lambda e, W0=W0: e.tensor_copy(W0[:, :, 1, :], bcast(cst[0:64, 0, 0:64], 1, 2)), r=[bcst], w=[bWl[0]])
                P.op("dve", lambda e: e.tensor_tensor(AM[:, :, 0, :], pa[0:64, 0, :, 64:128], MU, op=ALU.mult), r=[bPB[3], bmk], w=[bAM])
                P.op("dve", lambda e: e.tensor_tensor(AM[:, :, 1, :], pa[0:64, 1, :, 0:64], MSU, op=ALU.mult), r=[bPB[3], bmk], w=[bAM])
                P.op("dve", lambda e: e.tensor_tensor(AM[:, :, 2, :], pa[0:64, 1, :, 64:128], MU, op=ALU.mult), r=[bPB[3], bmk], w=[bAM])
                _part = _os.environ.get("PART", "Z")
                if _part == "A":
                    continue
                for q in range(3):
                    P.op("pe", lambda e, q=q, cs=cs: e.transpose(ptb[0:64, q * 128:(q + 1) * 128], BKV[:, q, cs], identb),
                         r=[bBKV, bcst], w=[bPB[6]])
                P.op("act", lambda e: e.copy(BKVt[:].rearrange("p q c -> p (q c)"), ptb[0:64, 0:384]), r=[bPB[6]], w=[bBKVt])
                yield
                if _part == "B":
                    continue
                for lvl in range(6):
                    Wc, bWc = Wl[lvl % 2], bWl[lvl % 2]
                    Wn, bWn = Wl[(lvl + 1) % 2], bWl[(lvl + 1) % 2]
                    for hh in range(2):
                        P.op("pe", lambda e, hh=hh, Wc=Wc: e.matmul(pi[0:64, hh, 0:128], Wc[:, hh, 2, :],
                                                                    Wc[:, hh, 0:2, :], start=True, stop=True), r=[bWc], w=[bPB[5]])
                        if lvl < 5:
                            P.op("pe", lambda e, hh=hh, Wc=Wc: e.matmul(pi[0:64, hh, 128:192], Wc[:, hh, 0, :], Wc[:, hh, 2, :],
                                                                        start=True, stop=True), r=[bWc], w=[bPB[5]])
                    if lvl < 5:
                        P.op("act", lambda e, Wn=Wn: e.copy(Wn[:, :, 0, :], pi[0:64, :, 0:64]), r=[bPB[5]], w=[bWn])
                        P.op("act", lambda e, Wn=Wn: e.copy(Wn[:, :, 2, :], pi[0:64, :, 128:192]), r=[bPB[5]], w=[bWn])
                    P.op("dve", lambda e, Wn=Wn, Wc=Wc: e.tensor_tensor(Wn[:, :, 1, :], pi[0:64, :, 64:128], Wc[:, :, 1, :], op=ALU.add),
                         r=[bPB[5], bWc], w=[bWn])
                    yield
                if _part == "C":
                    continue
                Wf, bWf = Wl[0], bWl[0]
                px = PB[7][:, 0:128].rearrange("p (h c) -> p h c", h=2)
                pu = PB[7][:, 128:256].rearrange("p (h c) -> p h c", h=2)
                py = PB[7][:, 256:384].rearrange("p (h c) -> p h c", h=2)
                pp = PB[4][:, 256:384]
                for hh in range(2):
                    hs = slice(64 * hh, 64 * hh + 64)
                    P.op("pe", lambda e, hh=hh, hs=hs, cs=cs: e.matmul(px[0:64, hh, :], AR[:, hh, 0, cs], STb[j][:, hs], start=True, stop=False),
                         r=[bAR, bSTb[j]], w=[bPB[7]])
                    P.op("pe", lambda e, hh=hh, hs=hs: e.matmul(px[0:64, hh, :], AM[:, hh, 1, :], BKVt[:, 2, hs], start=False, stop=True),
                         r=[bAM, bBKVt], w=[bPB[7]])
                P.op("act", lambda e: e.copy(Xb[:], px[0:64, :, :]), r=[bPB[7]], w=[bXb])
                for hh in range(2):
                    P.op("pe", lambda e, hh=hh: e.matmul(pu[0:64, hh, :], Wf[:, hh, 1, :], Xb[:, hh, :], start=True, stop=True),
                         r=[bWf, bXb], w=[bPB[7]])
                P.op("act", lambda e: e.copy(Ub[:], pu[0:64, :, :]), r=[bPB[7]], w=[bUb])
                for hh in range(2):
                    hs = slice(64 * hh, 64 * hh + 64)
                    P.op("pe", lambda e, hh=hh, hs=hs, cs=cs: e.matmul(py[0:64, hh, :], STb[j][:, hs], AR[:, hh, 1, cs], start=True, stop=False),
                         r=[bAR, bSTb[j]], w=[bPB[7]])
                    P.op("pe", lambda e, hh=hh: e.matmul(py[0:64, hh, :], Ub[:, hh, :], AM[:, hh, 0, :], start=False, stop=False),
                         r=[bAM, bUb], w=[bPB[7]])
                    P.op("pe", lambda e, hh=hh, hs=hs: e.matmul(py[0:64, hh, :], BKVt[:, 2, hs], AM[:, hh, 2, :], start=False, stop=True),
                         r=[bAM, bBKVt], w=[bPB[7]])
                P.op("act", lambda e, cs=cs: e.copy(YT[:, :, cs], py[0:64, :, :]), r=[bPB[7]], w=[bYT])
                P.op("pe", lambda e: e.matmul(pp, BKVt[:, 0, :], Ub[:].rearrange("p h c -> p (h c)"), start=True, stop=False),
                     r=[bBKVt, bUb], w=[bPB[4]])
                P.op("pe", lambda e: e.matmul(pp, BKVt[:, 1, :], BKVt[:, 2, :], start=False, stop=True), r=[bBKVt], w=[bPB[4]])
                P.op("dve", lambda e: e.tensor_tensor(tS[:], pp, ST[j][:], op=ALU.add), r=[bPB[4], bST[j]], w=[btS])
                P.op("dve", lambda e, c=c: e.tensor_scalar(ST[j][:], tS[:], GC[:, c:c + 1], None, op0=ALU.mult), r=[btS, bGC], w=[bST[j]])
                P.op("act", lambda e: e.copy(STb[j][:], ST[j][:]), r=[bST[j]], w=[bSTb[j]])
                yield
            mean, bmean = G["mean"]; msq, bmsq = G["msq"]; var, bvar = G["var"]; cen, bcen = G["cen"]; ysq, bysq = G["ysq"]
            vs, bvs = G["vs"]; bon, bbon = G["bon"]
            for hh in range(2):
                h = 2 * j + hh
                hs = slice(64 * hh, 64 * hh + 64)
                y_ = YT[:, hh, :]
                P.op("act", lambda e, y_=y_: e.activation(ysq[:], y_, AF.Square), r=[bYT], w=[bysq])
                P.op("pe", lambda e, y_=y_: e.matmul(PB[1][0:64, :], ones64f[:], y_, start=True, stop=True), r=[bo64, bYT], w=[bPB[1]])
                P.op("pe", lambda e: e.matmul(PB[2][0:64, :], ones64f[:], ysq[:], start=True, stop=True), r=[bo64, bysq], w=[bPB[2]])
                P.op("act", lambda e: e.mul(mean[:], PB[1][0:64, :], 1.0 / 64), r=[bPB[1]], w=[bmean])
                P.op("dve", lambda e: e.tensor_tensor(msq[:], mean[:], mean[:], op=ALU.mult), r=[bmean], w=[bmsq])
                P.op("dve", lambda e: e.scalar_tensor_tensor(var[:], PB[2][0:64, :], 1.0 / 64, msq[:], op0=ALU.mult, op1=ALU.subtract),
                     r=[bPB[2], bmsq], w=[bvar])
                P.op("act", lambda e: e.activation(var[:], var[:], AF.Sqrt, bias=64e-5), r=[bvar], w=[bvar])
                P.op("dve", lambda e: e.reciprocal(var[:], var[:]), r=[bvar], w=[bvar])
                P.op("dve", lambda e, y_=y_: e.tensor_tensor(cen[:], y_, mean[:], op=ALU.subtract), r=[bYT, bmean], w=[bcen])
                P.op("dve", lambda e: e.tensor_tensor(cen[:], cen[:], var[:], op=ALU.mult), r=[bcen, bvar], w=[bcen])
                P.op("act", lambda e, h=h: e.activation(cen[:], cen[:], AF.Identity, bias=lnb[:, h:h + 1], scale=lnw[:, h:h + 1]),
                     r=[bcen, bprm], w=[bcen])
                P.op("pe", lambda e, hs=hs, hh=hh: e.matmul(PB[1][0:64, :], C.ones_bf[:, 0:64], rkp[:, hh, :], start=True, stop=True),
                     r=[C.b_ones, brkp], w=[bPB[1]])
                P.op("pe", lambda e, hs=hs: e.matmul(PB[2][0:64, :], cst[:, 0, hs], BKV[:, 2, :], start=True, stop=True),
                     r=[bcst, bBKV], w=[bPB[2]])
                P.op("act", lambda e: e.copy(vs[:], PB[2][0:64, :]), r=[bPB[2]], w=[bvs])
                P.op("dve", lambda e: e.tensor_tensor(bon[:], PB[1][0:64, :], vs[:], op=ALU.mult), r=[bPB[1], bvs], w=[bbon])
                P.op("dve", lambda e: e.tensor_tensor(cen[:], cen[:], bon[:], op=ALU.add), r=[bcen, bbon], w=[bcen])
                P.op("pe", lambda e, h=h: e.matmul(PB[1][0:64, :], glo[:, h * 64:(h + 1) * 64], sdg[:], start=True, stop=True),
                     r=[bglo, bsdg], w=[bPB[1]])
                P.op("dve", lambda e, h=h: e.tensor_tensor(yall[:, h, :], cen[:], PB[1][0:64, :], op=ALU.mult), r=[bcen, bPB[1]], w=[byall])
                yield

        for j in range(4):
            for _ in rw_pair(j):
                pass
            if stop == "E":
                return

        rden, brden = G["rden"]
        for h in range(8):
            hp = 64 * (h % 2); pj = h // 2; hs = slice(hp, hp + 64)
            tz = tzb[h % 2]; bz = btz[h % 2]
            P.dma("pool", tz[:], tz_dram[h], w=[bz])
            nkt = 4 * i + 4
            for kt in range(nkt):
                k2 = kt % 2
                sc, bsc = PB[1 + k2], bPB[1 + k2]
                m0 = (i * 512 - kt * 128) + 384
                r_ = 8 * h + kt // 2
                P.op("pe", lambda e, hs=hs, pj=pj, kt=kt, sc=sc, h=h: e.matmul(sc[:], KT[:, pj, kt * 128:(kt + 1) * 128], QZ[:, pj, h % 2, :],
                                                                         start=True, stop=False), r=[bKT, bQZ], w=[bsc])
                P.op("pe", lambda e, r_=r_, sc=sc: e.matmul(sc[:], bcast(cst[0:64, 0, r_:r_ + 1], 1, 128)[:, :, 0] if False else
                                                            bass.AP(cst.tensor if hasattr(cst, "tensor") else cst, cst[0:64, 0, r_:r_ + 1].offset,
                                                                    [list(cst[0:64, 0, r_:r_ + 1].ap[0]), [0, 128]]),
                                                            NMT[:], start=False, stop=True), r=[bcst, bNMT], w=[bsc])
                P.op("act", lambda e, k2=k2, sc=sc: e.activation(PTb[k2][:], sc[:], AF.Exp, scale=0.125), r=[bsc], w=[bPTb[k2]])
                P.op("dve", lambda e, k2=k2, tz=tz, m0=m0: e.tensor_tensor(PTb[k2][:], PTb[k2][:], tz[:, m0:m0 + 512], op=ALU.mult),
                     r=[bPTb[k2], bz], w=[bPTb[k2]])
                P.op("pe", lambda e, k2=k2, kt=kt, h=h: e.matmul(PB[3][0:64, :], VC[:, kt, h * 64:(h + 1) * 64], PTb[k2][:],
                                                                 start=(kt == 0), stop=(kt == nkt - 1)), r=[bVC, bPTb[k2]], w=[bPB[3]])
                P.op("pe", lambda e, k2=k2, kt=kt: e.matmul(PB[4][0:64, :], C.ones_bf[:, 0:64], PTb[k2][:],
                                                            start=(kt == 0), stop=(kt == nkt - 1)), r=[C.b_ones, bPTb[k2]], w=[bPB[4]])
            P.op("dve", lambda e: e.reciprocal(rden[:], PB[4][0:64, :]), r=[bPB[4]], w=[brden])
            P.op("dve", lambda e, h=h: e.tensor_tensor(yall[:, 8 + h, :], PB[3][0:64, :], rden[:], op=ALU.mult), r=[bPB[3], brden], w=[byall])
        if stop == "F":
            return
        for m in range(8):
            k = m % 2
            pd, bpd = PB[5 + k], bPB[5 + k]
            for h in range(16):
                P.op("pe", lambda e, m=m, h=h, pd=pd: e.matmul(pd[:], wout[:, h, m * 128:(m + 1) * 128], yall[:, h, :],
                                                               start=(h == 0), stop=(h == 15)), r=[bwout, byall], w=[bpd])
            P.op("dve", lambda e, m=m, k=k, pd=pd: e.tensor_tensor(xs[k], pd[:], xf[:, m, :], op=ALU.add), r=[bpd, bxf], w=[bxs[k]])
            P.dma("sp", xout_v[:, m, t * TT:(t + 1) * TT], xs[k], r=[bxs[k]])


def make_t5oh():
    import jax, jax.numpy as jnp, math
    cpu = jax.devices("cpu")[0]
    with jax.default_device(cpu):
        dist = jnp.arange(NI, dtype=jnp.int32) - 511
        n = jnp.maximum(dist, 0)
        nf = jnp.maximum(n, 1).astype(jnp.float32)
        large = 16 + (jnp.log(nf / 16) / math.log(1024 / 16) * 16).astype(jnp.int32)
        bucket = np.asarray(jnp.where(n < 16, n, jnp.minimum(large, 31)))
        dist = np.asarray(dist)
    oh = np.zeros((33, NI), np.float32)
    for i_ in range(NI):
        if dist[i_] < 0:
            oh[32, i_] = 1.0
        else:
            oh[bucket[i_], i_] = 1.0
    return oh


from concourse.bass_utils import run_bass_kernel_spmd

W_NAMES = ["rel_bias",
           "l0_mix_norm", "l0_w_in", "l0_shift_mu", "l0_w0", "l0_w_lora_up", "l0_a0", "l0_a_lora_up", "l0_g_lora_up",
           "l0_k_k", "l0_k_a", "l0_r_k", "l0_lnx_w", "l0_lnx_b", "l0_q_norm", "l0_k_norm", "l0_w_out",
           "l0_ffn_norm", "l0_ffn_up", "l0_ffn_conv_w", "l0_ffn_conv_b", "l0_ffn_down",
           "l1_mix_norm", "l1_w_in", "l1_conv_w", "l1_w_out",
           "l1_ffn_norm", "l1_ffn_up", "l1_ffn_conv_w", "l1_ffn_conv_b", "l1_ffn_down"]


def build_program(shapes):
    nc = bass.Bass("TRN2", target_bir_lowering=False)
    P = Prog(nc); C = Ctx()
    x_tok = nc.dram_tensor("x_tok", [TOK, D], F32, kind="ExternalInput").ap()
    out_tok = nc.dram_tensor("out_tok", [TOK, D], F32, kind="ExternalOutput").ap()
    t5oh = nc.dram_tensor("t5oh", [33, NI], F32, kind="ExternalInput").ap()
    W = {n: nc.dram_tensor(n, list(shapes[n]), F32, kind="ExternalInput").ap() for n in W_NAMES}
    wb = nc.dram_tensor("wb_scr", [128, 8, 3328], BF16).ap()
    ev = nc.dram_tensor("ev_scr", [8, NI], BF16).ap()
    tz = nc.dram_tensor("tz_scr", [8, 128, EW], BF16).ap()
    xa = nc.dram_tensor("xa_scr", [D, TOK], F32).ap()
    xb = nc.dram_tensor("xb_scr", [D, TOK], F32).ap()
    xc = nc.dram_tensor("xc_scr", [D, TOK], F32).ap()
    mk_consts(P, C)
    P.barrier()
    glob = P.es

    def phase(fn):
        st = ExitStack(); P.es = st
        fn()
        st.close(); P.es = glob
        P.barrier()

    phase(lambda: phase_l0(P, C, x_tok, xa, W, t5oh, wb, ev, tz, "a0"))
    phase(lambda: phase_ffn(P, C, xa, xb, None, [W[n] for n in W_NAMES[17:22]], "f0"))
    phase(lambda: phase_l1mix(P, C, xb, xc, [W[n] for n in W_NAMES[22:26]], "m1"))
    phase(lambda: phase_ffn(P, C, xc, None, out_tok, [W[n] for n in W_NAMES[26:31]], "f1"))
    P.emit()
    return nc


def kernel(**inputs):
    x = np.ascontiguousarray(np.asarray(inputs["x"], dtype=np.float32))
    B = x.shape[0]
    ncores = 8
    per = B // ncores
    shapes = {n: np.asarray(inputs[n]).shape for n in W_NAMES}
    nc = build_program(shapes)
    oh = make_t5oh()
    in_maps = []
    for c in range(ncores):
        m = {"x_tok": np.ascontiguousarray(x[c * per:(c + 1) * per].reshape(TOK, D)), "t5oh": oh}
        for n in W_NAMES:
            m[n] = np.ascontiguousarray(np.asarray(inputs[n], dtype=np.float32))
        in_maps.append(m)
    res = run_bass_kernel_spmd(nc, in_maps, core_ids=list(range(ncores)))
    outs = [np.asarray(r["out_tok"]).reshape(per, T, D) for r in res.results]
    return np.concatenate(outs, axis=0).astype(np.float32)
```

```python
from contextlib import ExitStack
import numpy as np
import concourse.bass as bass
import concourse.mybir as mybir

F32 = mybir.dt.float32
BF16 = mybir.dt.bfloat16
AF = mybir.ActivationFunctionType
ALU = mybir.AluOpType
AX = mybir.AxisListType

COMPUTE = ("pe", "act", "dve", "pool")
EPOCH = 30000
DMA_POOL = 8


class Buf:
    __slots__ = ("name", "lw", "rd_eng", "rd_dma")

    def __init__(self, name=""):
        self.name = name
        self.lw = None
        self.rd_eng = {}
        self.rd_dma = []


class Op:
    __slots__ = ("eng", "fn", "deps", "has_dep", "sig", "clock", "waits", "is_dma", "idx")

    def __init__(self, eng, fn, is_dma):
        self.eng = eng
        self.fn = fn
        self.deps = ()
        self.has_dep = False
        self.sig = None
        self.clock = None
        self.waits = ()
        self.is_dma = is_dma


class Prog:
    def __init__(self, nc):
        self.nc = nc
        self.es = ExitStack()
        self.ops = []
        self.sems = {}
        self.nsem = 0
        self.final_dmas = []
        self.cur_barrier = None
        self.last_eng = {}
        self.dmas_since = []

    def sbuf(self, name, shape, dtype):
        t = self.es.enter_context(self.nc.sbuf_tensor(name, list(shape), dtype))
        return t

    def psum(self, name, shape, dtype):
        return self.es.enter_context(self.nc.psum_tensor(name, list(shape), dtype))

    def sem(self, key):
        if key not in self.sems:
            self.sems[key] = self.es.enter_context(self.nc.semaphore("s%d" % self.nsem))
            self.nsem += 1
        return self.sems[key]

    def _record(self, o, r, w):
        deps = set()
        for b in r:
            if b.lw is not None:
                deps.add(b.lw)
        for b in w:
            if b.lw is not None:
                deps.add(b.lw)
            deps.update(b.rd_eng.values())
            deps.update(b.rd_dma)
        if o.eng == "pe":
            deps = {d for d in deps if d.eng != "pe" or d.is_dma}
        if self.cur_barrier is not None:
            deps.add(self.cur_barrier)
        deps.discard(o)
        o.deps = tuple(deps)
        if o.is_dma:
            self.dmas_since.append(o)
        else:
            self.last_eng[o.eng] = o
        for b in r:
            if o.is_dma:
                b.rd_dma.append(o)
            else:
                b.rd_eng[o.eng] = o
        for b in w:
            b.lw = o
            b.rd_eng = {}
            b.rd_dma = []
        self.ops.append(o)
        return o

    def op(self, eng, fn, r=(), w=()):
        return self._record(Op(eng, fn, False), r, w)

    def dma(self, q, out, in_, r=(), w=(), final=False):
        o = Op(q, (lambda e, out=out, in_=in_: e.dma_start(out=out, in_=in_)), True)
        self._record(o, r, w)
        if final:
            self.final_dmas.append(o)
        return o

    def barrier(self):
        if not hasattr(self, "_dummy"):
            self._dummy = self.sbuf("bar_dummy", [128, 2], F32)
        d = self._dummy
        o = Op("pool", (lambda e: e.memset(d[:, 0:1], 0.0)), False)
        deps = set(self.last_eng.values()) | set(self.dmas_since)
        o.deps = tuple(deps)
        self.ops.append(o)
        self.cur_barrier = o
        self.last_eng = {"pool": o}
        self.dmas_since = []
        return o

    def emit(self):
        nc = self.nc
        ops = self.ops
        for o in ops:
            for d in o.deps:
                d.has_dep = True
        for o in self.final_dmas:
            o.has_dep = True
        engs = ("pe", "act", "dve", "pool", "sp")
        know = {e: {} for e in engs}
        cnt = {e: 0 for e in engs}
        dcnt = {e: 0 for e in engs}
        per_eng = {e: [] for e in engs}
        dma_prev = {}
        for o in ops:
            e = o.eng
            k = know[e]
            need = {}
            deps = list(o.deps)
            if o.is_dma:
                slot = dcnt[e] % DMA_POOL
                prev = dma_prev.get((e, slot))
                if prev is not None:
                    deps.append(prev)
            for d in deps:
                key, val = d.sig
                if k.get(key, 0) < val:
                    if need.get(key, (0, None))[0] < val:
                        need[key] = (val, d)
            waits = []
            for key, (val, d) in need.items():
                waits.append((key, val))
                for kk, vv in d.clock.items():
                    if k.get(kk, 0) < vv:
                        k[kk] = vv
            o.waits = waits
            if o.is_dma:
                j = dcnt[e]
                dcnt[e] += 1
                key = ("dma", e, j % DMA_POOL)
                o.sig = (key, 16 * (j // DMA_POOL + 1))
                dma_prev[(e, j % DMA_POOL)] = o
                o.clock = dict(k)
                o.clock[key] = o.sig[1]
            elif o.has_dep:
                cnt[e] += 1
                ep, v = divmod(cnt[e] - 1, EPOCH)
                key = ("c", e, ep)
                o.sig = (key, v + 1)
                o.clock = dict(k)
                o.clock[key] = v + 1
            per_eng[e].append(o)
        for o in ops:
            if o.sig is not None:
                self.sem(o.sig[0])
        final_waits = [o.sig for o in self.final_dmas]
        nc_eng = {"pe": "tensor", "act": "scalar", "dve": "vector", "pool": "gpsimd", "sp": "sync"}
        with nc.Block() as block:
            def make(e):
                def body(eng):
                    for o in per_eng[e]:
                        for key, val in o.waits:
                            eng.wait_ge(self.sems[key], val)
                        ins = o.fn(eng)
                        if o.sig is not None:
                            ins.then_inc(self.sems[o.sig[0]], 16 if o.is_dma else 1)
                    if e == "sp":
                        for key, val in final_waits:
                            eng.wait_ge(self.sems[key], val)
                return body
            for e in engs:
                if per_eng[e] or e == "sp":
                    getattr(block, nc_eng[e])(make(e))
        n = {e: len(per_eng[e]) for e in engs}
        return n

    def close(self):
        self.es.close()


D = 1024; T = 2048; NB = 2; TOK = NB * T; TT = 512; NT = TOK // TT; DFF = 2816; NH = 22


class Ctx:
    pass


def mk_consts(P, C):
    nc = P.nc
    C.ident = P.sbuf("ident", [128, 128], F32); C.b_ident = Buf()
    P.op("pool", lambda e: e.memset(C.ident[:], 1.0), w=[C.b_ident])
    P.op("pool", lambda e: e.affine_select(C.ident[:], C.ident[:], pattern=[[-1, 128]], compare_op=ALU.is_equal,
                                           fill=0.0, base=0, channel_multiplier=1), r=[C.b_ident], w=[C.b_ident])
    C.ones_bf = P.sbuf("ones_bf", [128, 128], BF16); C.b_ones = Buf()
    P.op("pool", lambda e: e.memset(C.ones_bf[:], 1.0), w=[C.b_ones])
    C.cast_i = 0
    C.colstg = P.sbuf("colstg", [64, 128], F32); C.b_colstg = Buf()


def load_w_bf16(P, C, w_dram, dst, bdst, K, N, stg, bstg):
    kc = K // 128
    wv = w_dram.rearrange("(c p) n -> p c n", p=128)
    CW = stg[0].shape[1]
    for c in range(kc):
        for n0 in range(0, N, CW):
            n1 = min(N, n0 + CW)
            i = C.cast_i % len(stg)
            C.cast_i += 1
            st, bs = stg[i], bstg[i]
            P.dma("sp", st[:, 0:n1 - n0], wv[:, c, n0:n1], w=[bs])
            eng = ("act", "pool", "dve")[C.cast_i % 3] if False else ("act", "pool")[C.cast_i % 2]
            if eng == "act":
                P.op("act", lambda e, st=st, c=c, n0=n0, n1=n1: e.copy(dst[:, c, n0:n1], st[:, 0:n1 - n0]), r=[bs], w=[bdst])
            else:
                P.op(eng, lambda e, st=st, c=c, n0=n0, n1=n1: e.tensor_copy(dst[:, c, n0:n1], st[:, 0:n1 - n0]), r=[bs], w=[bdst])


def load_cols(P, C, v_dram, dst, bdst, n, ps, bps):
    P.dma("pool", C.colstg[0:n, :], v_dram.rearrange("(c p) -> c p", p=128), w=[C.b_colstg])
    P.op("pe", lambda e: e.transpose(ps[:, 0:n], C.colstg[0:n, :], C.ident[0:n, 0:n]), r=[C.b_colstg, C.b_ident], w=[bps])
    P.op("act", lambda e: e.copy(dst, ps[:, 0:n]), r=[bps], w=[bdst])


def rms_norm_tile(P, C, xf, bxf, g, bg, xn, bxn, sq, bsq, ps_ss, bps, rstd, brstd):
    for c in range(8):
        P.op("act", lambda e, c=c: e.activation(sq[:, c, :], xf[:, c, :], AF.Square), r=[bxf], w=[bsq])
    for c in range(8):
        P.op("pe", lambda e, c=c: e.matmul(ps_ss[:], C.ones_bf[:], sq[:, c, :], start=(c == 0), stop=(c == 7)),
             r=[bsq, C.b_ones], w=[bps])
    P.op("act", lambda e: e.activation(rstd[:], ps_ss[:], AF.Sqrt, bias=1e-6, scale=1.0 / D), r=[bps], w=[brstd])
    P.op("dve", lambda e: e.reciprocal(rstd[:], rstd[:]), r=[brstd], w=[brstd])
    for c in range(8):
        P.op("dve", lambda e, c=c: e.scalar_tensor_tensor(xn[:, c, :], xf[:, c, :], g[:, c:c + 1], rstd[:],
                                                          op0=ALU.mult, op1=ALU.mult), r=[bxf, bg, brstd], w=[bxn])


def conv3(P, src_ps, bsrc, zb, bzb, acc, bacc, w3, bw3, ch, bias, hz, bhz, first, eng_copy="act"):
    if first:
        P.op("pool", lambda e: e.memset(zb[:, 0:2], 0.0), w=[bzb])
    else:
        P.op("pool", lambda e: e.tensor_copy(zb[:, 0:2], hz[:, ch, :]), r=[bhz], w=[bzb])
    P.op("act", lambda e: e.copy(zb[:, 2:514], src_ps), r=[bsrc], w=[bzb])
    if bias is not None:
        P.op("act", lambda e: e.activation(acc[:], src_ps, AF.Identity, bias=bias, scale=w3[:, 2, ch:ch + 1]),
             r=[bsrc, bw3], w=[bacc])
    else:
        P.op("act", lambda e: e.activation(acc[:], src_ps, AF.Copy, scale=w3[:, 2, ch:ch + 1]),
             r=[bsrc, bw3], w=[bacc])
    P.op("dve", lambda e: e.scalar_tensor_tensor(acc[:], zb[:, 1:513], w3[:, 1, ch:ch + 1], acc[:],
                                                 op0=ALU.mult, op1=ALU.add), r=[bzb, bw3, bacc], w=[bacc])
    P.op("dve", lambda e: e.scalar_tensor_tensor(acc[:], zb[:, 0:512], w3[:, 0, ch:ch + 1], acc[:],
                                                 op0=ALU.mult, op1=ALU.add), r=[bzb, bw3, bacc], w=[bacc])
    P.op("pool", lambda e: e.tensor_copy(hz[:, ch, :], zb[:, 512:514]), r=[bzb], w=[bhz])


def phase_ffn(P, C, x_in, x_out, out_tok, ws, tag):
    nc = P.nc
    norm_w, up, conv_w, conv_b, down = ws
    S = lambda n, sh, dt: P.sbuf(tag + n, sh, dt)
    wup = S("wup", [128, 8, 2 * DFF], BF16); bwup = Buf()
    wdn = S("wdn", [128, NH, D], BF16); bwdn = Buf()
    stg = [S("stg%d" % i, [128, 512], F32) for i in range(2)]; bstg = [Buf(), Buf()]
    g = S("g", [128, 8], F32); bg = Buf()
    w3 = S("w3", [128, 3, 44], F32); bw3 = Buf()
    cb = S("cb", [128, 44], F32); bcb = Buf()
    load_w_bf16(P, C, up, wup, bwup, D, 2 * DFF, stg, bstg)
    load_w_bf16(P, C, down, wdn, bwdn, DFF, D, stg, bstg)
    xf1 = S("xf", [128, 8, TT], F32); bxf1 = Buf()
    xf = [xf1, xf1]; bxf = [bxf1, bxf1]
    xn = S("xn", [128, 8, TT], BF16); bxn = Buf()
    rstd = S("rstd", [128, TT], F32); brstd = Buf()
    hb = S("hb", [128, NH, TT], BF16); bhb = Buf()
    sq = hb; bsq = bhb
    hz = S("hz", [128, 44, 2], F32); bhz = Buf()
    def one(n, w):
        t_ = S(n, [128, w], F32); b_ = Buf()
        return [t_, t_], [b_, b_]
    za, bza = one("za", 514); zu, bzu = one("zu", 514)
    aa, baa = one("aa", TT); au, bau = one("au", TT); sa, bsa = one("sa", TT)
    xo = [S("xo%d" % i, [128, TT], F32) for i in range(2)]; bxo = [Buf(), Buf()]
    psA = [P.psum(tag + "psA%d" % i, [128, TT], F32) for i in range(2)]; bpsA = [Buf(), Buf()]
    psU = [P.psum(tag + "psU%d" % i, [128, TT], F32) for i in range(2)]; bpsU = [Buf(), Buf()]
    psS = P.psum(tag + "psS", [128, TT], F32); bpsS = Buf()
    psD = [P.psum(tag + "psD%d" % i, [128, TT], F32) for i in range(2)]; bpsD = [Buf(), Buf()]
    load_cols(P, C, norm_w, g[:, 0:8], bg, 8, psS, bpsS)
    for k in range(3):
        load_cols(P, C, conv_w[k], w3[:, k, :], bw3, 44, psS, bpsS)
    load_cols(P, C, conv_b, cb[:, 0:44], bcb, 44, psS, bpsS)
    if out_tok is not None:
        pst = P.psum(tag + "psT", [128, TT], F32); bpst = Buf()
        ob = [S("ob%d" % i, [128, TT], F32) for i in range(2)]; bob = [Buf(), Buf()]
        otv = out_tok.rearrange("(n s p) d -> n p s d", p=128, s=4)
    xin_v = x_in.rearrange("(c p) t -> p c t", p=128)
    xout_v = x_out.rearrange("(c p) t -> p c t", p=128) if x_out is not None else None
    for t in range(NT):
        first = (t % (T // TT) == 0)
        X, bX = xf[t % 2], bxf[t % 2]
        P.dma("pool", X[:], xin_v[:, :, t * TT:(t + 1) * TT], w=[bX])
        rms_norm_tile(P, C, X, bX, g, bg, xn, bxn, sq, bsq, psS, bpsS, rstd, brstd)
        for j in range(NH):
            k = j % 2
            for c in range(8):
                P.op("pe", lambda e, c=c, j=j, k=k: e.matmul(psA[k][:], wup[:, c, j * 128:(j + 1) * 128], xn[:, c, :],
                                                             start=(c == 0), stop=(c == 7)), r=[bwup, bxn], w=[bpsA[k]])
            for c in range(8):
                P.op("pe", lambda e, c=c, j=j, k=k: e.matmul(psU[k][:], wup[:, c, DFF + j * 128:DFF + (j + 1) * 128], xn[:, c, :],
                                                             start=(c == 0), stop=(c == 7)), r=[bwup, bxn], w=[bpsU[k]])
            conv3(P, psA[k][:], bpsA[k], za[k], bza[k], aa[k], baa[k], w3, bw3, j, cb[:, j:j + 1], hz, bhz, first)
            conv3(P, psU[k][:], bpsU[k], zu[k], bzu[k], au[k], bau[k], w3, bw3, NH + j, cb[:, NH + j:NH + j + 1], hz, bhz, first)
            P.op("act", lambda e, k=k: e.activation(sa[k][:], aa[k][:], AF.Silu), r=[baa[k]], w=[bsa[k]])
            P.op("dve", lambda e, k=k, j=j: e.tensor_tensor(hb[:, j, :], sa[k][:], au[k][:], op=ALU.mult),
                 r=[bsa[k], bau[k]], w=[bhb])
        for m in range(8):
            k = m % 2
            XO, bXO = xo[k], bxo[k]
            for j in range(NH):
                P.op("pe", lambda e, m=m, j=j, k=k: e.matmul(psD[k][:], wdn[:, j, m * 128:(m + 1) * 128], hb[:, j, :],
                                                             start=(j == 0), stop=(j == NH - 1)), r=[bwdn, bhb], w=[bpsD[k]])
            P.op("dve", lambda e, m=m, k=k, XO=XO, X=X: e.tensor_tensor(XO[:], psD[k][:], X[:, m, :], op=ALU.add),
                 r=[bpsD[k], bX], w=[bXO])
            if xout_v is not None:
                P.dma("sp", xout_v[:, m, t * TT:(t + 1) * TT], XO[:], r=[bXO])
            if out_tok is not None:
                for s in range(4):
                    P.op("pe", lambda e, s=s, XO=XO: e.transpose(pst[:, s * 128:(s + 1) * 128], XO[:, s * 128:(s + 1) * 128], C.ident[:]),
                         r=[bXO, C.b_ident], w=[bpst])
                P.op("act", lambda e, k=k: e.copy(ob[k][:], pst[:]), r=[bpst], w=[bob[k]])
                P.dma("sp", otv[t, :, :, m * 128:(m + 1) * 128], ob[k][:].rearrange("p (s f) -> p s f", s=4), r=[bob[k]], final=True)


def phase_l1mix(P, C, x_in, x_out, ws, tag):
    norm_w, w_in, conv_w, w_out = ws
    S = lambda n, sh, dt: P.sbuf(tag + n, sh, dt)
    win = S("win", [128, 8, 3 * D], BF16); bwin = Buf()
    wout = S("wout", [128, 8, D], BF16); bwout = Buf()
    stg = [S("stg%d" % i, [128, 2048], F32) for i in range(2)]; bstg = [Buf(), Buf()]
    g = S("g", [128, 8], F32); bg = Buf()
    w3 = S("w3", [128, 3, 8], F32); bw3 = Buf()
    load_w_bf16(P, C, w_in, win, bwin, D, 3 * D, stg, bstg)
    load_w_bf16(P, C, w_out, wout, bwout, D, D, stg, bstg)
    xf = [S("xf%d" % i, [128, 8, TT], F32) for i in range(2)]; bxf = [Buf(), Buf()]
    xn = S("xn", [128, 8, TT], BF16); bxn = Buf()
    sq = S("sq", [128, 8, TT], BF16); bsq = Buf()
    rstd = S("rstd", [128, TT], F32); brstd = Buf()
    gm = S("gm", [128, 8, TT], BF16); bgm = Buf()
    hz = S("hz", [128, 8, 2], F32); bhz = Buf()
    zc = [S("zc%d" % i, [128, 514], F32) for i in range(2)]; bzc = [Buf(), Buf()]
    cs = [S("cs%d" % i, [128, TT], F32) for i in range(2)]; bcs = [Buf(), Buf()]
    chh = [S("ch%d" % i, [128, TT], F32) for i in range(2)]; bchh = [Buf(), Buf()]
    ac = [S("ac%d" % i, [128, TT], F32) for i in range(2)]; bac = [Buf(), Buf()]
    xo = [S("xo%d" % i, [128, 8, TT], F32) for i in range(2)]; bxo = [Buf(), Buf()]
    psB = [P.psum(tag + "psB%d" % i, [128, TT], F32) for i in range(2)]; bpsB = [Buf(), Buf()]
    psC = [P.psum(tag + "psC%d" % i, [128, TT], F32) for i in range(2)]; bpsC = [Buf(), Buf()]
    psH = [P.psum(tag + "psH%d" % i, [128, TT], F32) for i in range(2)]; bpsH = [Buf(), Buf()]
    psS = P.psum(tag + "psS", [128, TT], F32); bpsS = Buf()
    psD = P.psum(tag + "psD", [128, TT], F32); bpsD = Buf()
    load_cols(P, C, norm_w, g[:, 0:8], bg, 8, psS, bpsS)
    for k in range(3):
        load_cols(P, C, conv_w[k], w3[:, k, :], bw3, 8, psS, bpsS)
    xin_v = x_in.rearrange("(c p) t -> p c t", p=128)
    xout_v = x_out.rearrange("(c p) t -> p c t", p=128)
    for t in range(NT):
        first = (t % (T // TT) == 0)
        X, bX = xf[t % 2], bxf[t % 2]
        P.dma("pool", X[:], xin_v[:, :, t * TT:(t + 1) * TT], w=[bX])
        rms_norm_tile(P, C, X, bX, g, bg, xn, bxn, sq, bsq, psS, bpsS, rstd, brstd)
        for j in range(8):
            k = j % 2
            for (ps, bps, off) in ((psB[k], bpsB[k], 0), (psC[k], bpsC[k], D), (psH[k], bpsH[k], 2 * D)):
                for c in range(8):
                    P.op("pe", lambda e, c=c, j=j, ps=ps, off=off: e.matmul(ps[:], win[:, c, off + j * 128:off + (j + 1) * 128],
                                                                            xn[:, c, :], start=(c == 0), stop=(c == 7)),
                         r=[bwin, bxn], w=[bps])
            P.op("act", lambda e, k=k: e.copy(cs[k][:], psC[k][:]), r=[bpsC[k]], w=[bcs[k]])
            P.op("dve", lambda e, k=k: e.tensor_tensor(chh[k][:], cs[k][:], psH[k][:], op=ALU.mult),
                 r=[bcs[k], bpsH[k]], w=[bchh[k]])
            conv3(P, chh[k][:], bchh[k], zc[k], bzc[k], ac[k], bac[k], w3, bw3, j, None, hz, bhz, first)
            P.op("dve", lambda e, k=k, j=j: e.tensor_tensor(gm[:, j, :], ac[k][:], psB[k][:], op=ALU.mult),
                 r=[bac[k], bpsB[k]], w=[bgm])
        XO, bXO = xo[t % 2], bxo[t % 2]
        for m in range(8):
            for j in range(8):
                P.op("pe", lambda e, m=m, j=j: e.matmul(psD[:], wout[:, j, m * 128:(m + 1) * 128], gm[:, j, :],
                                                        start=(j == 0), stop=(j == 7)), r=[bwout, bgm], w=[bpsD])
            P.op("dve", lambda e, m=m, XO=XO, X=X: e.tensor_tensor(XO[:, m, :], psD[:], X[:, m, :], op=ALU.add),
                 r=[bpsD, bX], w=[bXO])
        P.dma("sp", xout_v[:, :, t * TT:(t + 1) * TT], XO[:], r=[bXO])


RW_COLS = 1792
EW = 2816
NI = EW + 127
CDEC = 0.6065306597126334


def bcast(ap, pos, count):
    l = [list(x) for x in ap.ap]
    l.insert(pos, [0, count])
    return bass.AP(ap.tensor, ap.offset, l)


def run_interleaved(gens):
    gens = list(gens)
    while gens:
        for g_ in list(gens):
            try:
                next(g_)
            except StopIteration:
                gens.remove(g_)


def phase_l0(P, C, x_tok, x_out, W, t5oh, wb_dram, ev_dram, tz_dram, tag, stop=None):
    nc = P.nc
    S = lambda n, sh, dt: P.sbuf(tag + n, sh, dt)
    def SB(n, sh, dt):
        return S(n, sh, dt), Buf()
    PB = [P.psum(tag + "pb%d" % i, [128, 512], F32) for i in range(8)]; bPB = [Buf() for _ in range(8)]
    xf, bxf = SB("xf", [128, 8, TT], F32)
    stg = [xf[:, 0:2, :].rearrange("p a b -> p (a b)"), xf[:, 2:4, :].rearrange("p a b -> p (a b)")]
    stgb = [xf[:, 4:5, :].rearrange("p a b -> p (a b)").bitcast(BF16), xf[:, 5:6, :].rearrange("p a b -> p (a b)").bitcast(BF16)]
    bstg = [Buf(), Buf()]; bstgb = [Buf(), Buf()]
    wv = W["l0_w_in"].rearrange("(c p) n -> p c n", p=128)
    i_ = 0
    for c in range(8):
        for n0 in range(0, 3328, 1024):
            n1 = min(3328, n0 + 1024); k = i_ % 2; i_ += 1
            P.dma("sp", stg[k][:, 0:n1 - n0], wv[:, c, n0:n1], w=[bstg[k]])
            P.op("act" if k == 0 else "pool",
                 (lambda e, k=k, n=n1 - n0: e.copy(stgb[k][:, 0:n], stg[k][:, 0:n])) if k == 0 else
                 (lambda e, k=k, n=n1 - n0: e.tensor_copy(stgb[k][:, 0:n], stg[k][:, 0:n])), r=[bstg[k]], w=[bstgb[k]])
            P.dma("pool", wb_dram[:, c, n0:n1], stgb[k][:, 0:n1 - n0], r=[bstgb[k]])
    wout, bwout = SB("wout", [64, 16, D], BF16)
    wo_v = W["l0_w_out"].rearrange("(h p) n -> p h n", p=64)
    for h in range(16):
        k = h % 2
        P.dma("sp", stg[k][0:64, 0:1024], wo_v[:, h, :], w=[bstg[k]])
        P.op("act", lambda e, k=k, h=h: e.copy(wout[:, h, :], stg[k][0:64, 0:1024]), r=[bstg[k]], w=[bwout])
    lwa, blwa = SB("lwa", [128, 512], BF16)
    glo, bglo = SB("glo", [128, 512], BF16)
    P.dma("sp", stg[0][0:64, 0:512], W["l0_w_lora_up"], w=[bstg[0]])
    P.dma("sp", stg[0][64:128, 0:512], W["l0_a_lora_up"], w=[bstg[0]])
    P.op("act", lambda e: e.copy(lwa[:], stg[0][:, 0:512]), r=[bstg[0]], w=[blwa])
    P.dma("sp", stg[1][:, 0:512], W["l0_g_lora_up"], w=[bstg[1]])
    P.op("act", lambda e: e.copy(glo[:], stg[1][:, 0:512]), r=[bstg[1]], w=[bglo])
    prm, bprm = SB("prm", [128, 64], F32)
    g_ = prm[:, 0:8]; mu = prm[:, 8:22]; w0 = prm[:, 22:26]; a0 = prm[:, 26:30]; kk_ = prm[:, 30:34]
    ka_ = prm[:, 34:38]; rk_ = prm[:, 38:42]; qn = prm[:, 42:43]; kn = prm[:, 43:44]
    lnw = prm[0:64, 44:52]; lnb = prm[0:64, 52:60]
    load_cols(P, C, W["l0_mix_norm"], g_, bprm, 8, PB[0], bPB[0])
    load_cols(P, C, W["l0_shift_mu"], mu, bprm, 14, PB[0], bPB[0])
    load_cols(P, C, W["l0_w0"], w0, bprm, 4, PB[0], bPB[0])
    load_cols(P, C, W["l0_a0"], a0, bprm, 4, PB[0], bPB[0])
    load_cols(P, C, W["l0_k_k"], kk_, bprm, 4, PB[0], bPB[0])
    load_cols(P, C, W["l0_k_a"], ka_, bprm, 4, PB[0], bPB[0])
    load_cols(P, C, W["l0_r_k"].rearrange("h d -> (h d)"), rk_, bprm, 4, PB[0], bPB[0])
    for (src, dst) in ((W["l0_q_norm"], qn), (W["l0_k_norm"], kn)):
        P.dma("pool", C.colstg[0:1, 0:64], src.rearrange("(o p) -> o p", o=1), w=[C.b_colstg])
        P.dma("pool", C.colstg[0:1, 64:128], src.rearrange("(o p) -> o p", o=1), w=[C.b_colstg])
        P.op("pe", lambda e: e.transpose(PB[0][:, 0:1], C.colstg[0:1, :], C.ident[0:1, 0:1]), r=[C.b_colstg, C.b_ident], w=[bPB[0]])
        P.op("act", lambda e, dst=dst: e.copy(dst, PB[0][:, 0:1]), r=[bPB[0]], w=[bprm])
    for (src, dst) in ((W["l0_lnx_w"], lnw), (W["l0_lnx_b"], lnb)):
        P.dma("pool", C.colstg[0:8, 0:64], src.rearrange("(h p) -> h p", p=64), w=[C.b_colstg])
        P.op("pe", lambda e: e.transpose(PB[0][0:64, 0:8], C.colstg[0:8, 0:64], C.ident[0:8, 0:8]), r=[C.b_colstg, C.b_ident], w=[bPB[0]])
        P.op("act", lambda e, dst=dst: e.copy(dst, PB[0][0:64, 0:8]), r=[bPB[0]], w=[bprm])
    if stop == "A":
        return
    cst, bcst = SB("cstb", [128, 3, 128], BF16)
    identb = cst[:, 0, :]; Jb = cst[:, 1, :]; BD = cst[:, 2, :]
    cf, bcf = SB("cstf", [128, 128], F32)
    P.op("pool", lambda e: e.memset(cf[:], 1.0), w=[bcf])
    P.op("pool", lambda e: e.affine_select(cf[:], cf[:], pattern=[[1, 128]], compare_op=ALU.is_equal, fill=0.0, base=-127,
                                           channel_multiplier=1), r=[bcf], w=[bcf])
    P.op("pool", lambda e: e.tensor_copy(Jb, cf[:]), r=[bcf], w=[bcst])
    P.op("pool", lambda e: e.tensor_copy(identb, C.ident[:]), r=[C.b_ident], w=[bcst])
    P.op("pool", lambda e: e.memset(BD, 0.0), w=[bcst])
    P.op("pool", lambda e: e.memset(cst[0:64, 2, 0:64], 1.0), r=[bcst], w=[bcst])
    P.op("pool", lambda e: e.memset(cst[64:128, 2, 64:128], 1.0), r=[bcst], w=[bcst])
    ones64f, bo64 = SB("ones64f", [64, 64], F32)
    P.op("pool", lambda e: e.memset(ones64f[:], 1.0), w=[bo64])
    MS, bMS = SB("MS", [128, 512], F32)
    P.op("pool", lambda e: e.memset(MS[:], 1.0), w=[bMS])
    P.op("pool", lambda e: e.memset(MS[:].rearrange("p (c t) -> p c t", t=64)[:, :, 0:1], 0.0), r=[bMS], w=[bMS])
    mk, bmk = SB("mk", [64, 3, 2, 64], F32)
    P.op("pool", lambda e: e.memset(mk[:], 1.0), w=[bmk])
    for hh in range(2):
        P.op("pool", lambda e, hh=hh: e.affine_select(mk[:, 0, hh, :], mk[:, 0, hh, :], pattern=[[1, 64]], compare_op=ALU.is_ge,
                                                      fill=0.0, base=-1, channel_multiplier=-1), r=[bmk], w=[bmk])
        P.op("pool", lambda e, hh=hh: e.affine_select(mk[:, 1, hh, :], mk[:, 1, hh, :], pattern=[[1, 64]], compare_op=ALU.is_ge,
                                                      fill=0.0, base=0, channel_multiplier=-1), r=[bmk], w=[bmk])
        P.op("pool", lambda e, hh=hh: e.affine_select(mk[:, 2, hh, :], mk[:, 2, hh, :], pattern=[[-1, 64]], compare_op=ALU.is_ge,
                                                      fill=0.0, base=-1, channel_multiplier=1), r=[bmk], w=[bmk])
    MSU = mk[:, 0, :, :]; MU = mk[:, 1, :, :]; MSL = mk[:, 2, :, :]
    pm, bpm = SB("pm", [128, 3, 8, 8], F32)
    P.op("pool", lambda e: e.memset(pm[:], 0.0), w=[bpm])
    for n in range(8):
        if n > 0:
            P.op("pool", lambda e, n=n: e.memset(pm[:, 0, n, 0:n], 1.0), r=[bpm], w=[bpm])
        P.op("pool", lambda e, n=n: e.memset(pm[:, 1, n, n:n + 1], 1.0), r=[bpm], w=[bpm])
    P.op("pool", lambda e: e.tensor_scalar(pm[:, 2, :, :], pm[:, 0, :, :], -1.0, 1e30, op0=ALU.add, op1=ALU.mult), r=[bpm], w=[bpm])
    if stop == "A2":
        return
    _main_es = P.es
    _sub = ExitStack(); P.es = _sub
    relb, brelb = SB("relb", [33, 8], F32)
    P.op("pool", lambda e: e.memset(relb[32:33, :], -30000.0), w=[brelb])
    P.dma("pool", relb[0:32, :], W["rel_bias"], r=[brelb], w=[brelb])
    oh, boh = SB("oh", [33, 512], F32)
    evb, bevb = SB("evb", [8, 512], BF16)
    for n0 in range(0, NI, 512):
        n1 = min(NI, n0 + 512)
        P.dma("pool", oh[:, 0:n1 - n0], t5oh[:, n0:n1], w=[boh])
        P.op("pe", lambda e, n=n1 - n0: e.matmul(PB[1][0:8, 0:n], relb[:], oh[:, 0:n], start=True, stop=True), r=[brelb, boh], w=[bPB[1]])
        P.op("act", lambda e, n=n1 - n0: e.activation(evb[:, 0:n], PB[1][0:8, 0:n], AF.Exp), r=[bPB[1]], w=[bevb])
        P.dma("pool", ev_dram[:, n0:n1], evb[:, 0:n1 - n0], r=[bevb])
    P.barrier()
    _sub.close(); P.es = _main_es
    tzb = [S("tz%d" % i, [128, EW], BF16) for i in range(2)]; btz = [Buf(), Buf()]
    for h in range(8):
        hk = tzb[0]
        src = bass.AP(ev_dram.tensor, h * NI, [[1, 128], [1, EW]])
        P.dma("pool", hk[:], src, w=[btz[0]])
        for n0 in range(0, EW, 512):
            n1 = min(EW, n0 + 512)
            P.op("pe", lambda e, n0=n0, n1=n1: e.matmul(PB[1][:, 0:n1 - n0], Jb, tzb[0][:, n0:n1], start=True, stop=True),
                 r=[bcst, btz[0]], w=[bPB[1]])
            P.op("act", lambda e, n0=n0, n1=n1: e.copy(tzb[1][:, n0:n1], PB[1][:, 0:n1 - n0]), r=[bPB[1]], w=[btz[1]])
        P.dma("pool", tz_dram[h], tzb[1][:], r=[btz[1]])
    P.barrier()
    if stop == "B":
        return
    KT, bKT = SB("KT", [128, 4, T], BF16)
    VC, bVC = SB("VC", [128, 16, 512], BF16)
    kmT, bkmT = SB("kmT", [128, 4, 8], BF16)
    kmf, bkmf = SB("kmf", [128, 4, 8], F32)
    P.op("pool", lambda e: e.memset(kmf[:], 0.0), w=[bkmf])
    HP, bHP = SB("HP", [128, 14], F32)
    ST = [S("ST%d" % j, [128, 128], F32) for j in range(4)]; bST = [Buf() for _ in range(4)]
    STb = [S("STb%d" % j, [128, 128], BF16) for j in range(4)]; bSTb = [Buf() for _ in range(4)]
    xn, bxn = SB("xn", [128, 8, TT], BF16)
    yall, byall = SB("yall", [64, 16, TT], BF16)
    rstd, brstd = SB("rstd", [128, TT], F32)
    xt1 = S("xt", [128, D], F32); bxt1 = Buf()
    xt = [xt1, xt1]; bxt = [bxt1, bxt1]
    wr = [S("wr%d" % i, [128, 8, 128], BF16) for i in range(2)]; bwr = [Buf() for _ in range(2)]
    wvv, bwvv = SB("wvv", [128, 8, 512], BF16)
    sq = wvv; bsq = bwvv
    QZ, bQZ = SB("QZ", [128, 4, 2, TT], BF16)
    NMT, bNMT = SB("NMT", [64, TT], BF16)
    tda, btda = SB("tdaZ", [128, 2, TT], BF16)
    sdg, bsdg = SB("sdg", [128, TT], BF16)
    praw, bpraw = SB("praw", [128, 513], F32)
    dif, bdif = SB("dif", [128, TT], F32)
    F = {}
    for n in ("r", "k", "v", "sg", "a", "lcs", "t1", "e2", "e3", "kp"):
        F[n] = SB("f_" + n, [128, TT], F32)
    F["e1"] = F["t1"]; F["kk0"] = F["sg"]; F["rn"] = (dif, bdif)
    AR, bAR = SB("ARZ", [128, 2, 2, TT], BF16)
    BKV, bBKV = SB("BKV", [128, 3, TT], BF16)
    rkp, brkp = SB("rkpZ", [128, 2, TT], BF16)
    for (zt, bz_) in ((QZ, bQZ), (tda, btda), (AR, bAR), (rkp, brkp)):
        P.op("pool", lambda e, zt=zt: e.memset(zt[:], 0.0), w=[bz_])
    sqk, bsqk = SB("sqk", [128, TT], BF16)
    GC, bGC = SB("GC", [128, 8], F32)
    WlS = [[S("Wl%d_%d" % (s_, i), [64, 2, 3, 64], BF16) for i in range(2)] for s_ in range(2)]
    bWlS = [[Buf(), Buf()], [Buf(), Buf()]]
    AMS = [S("AM%d" % s_, [64, 2, 3, 64], BF16) for s_ in range(2)]; bAMS = [Buf(), Buf()]
    BKVtS = [S("BKVt%d" % s_, [64, 3, 128], BF16) for s_ in range(2)]; bBKVtS = [Buf(), Buf()]
    bpn = Buf(); bpp = Buf()
    Xb, bXb = SB("Xb", [64, 2, 64], BF16)
    Ub, bUb = SB("Ub", [64, 2, 64], BF16)
    YT, bYT = SB("YT", [64, 2, TT], F32)
    tS, btS = SB("tS", [128, 128], F32)
    G = {}
    for n, fn_ in (("mean", "sg"), ("msq", "lcs"), ("var", "t1"), ("cen", "e2"), ("ysq", "e3"), ("vs", "kp"), ("bon", "a"), ("rden", "rn")):
        G[n] = (F[fn_][0][0:64, :], F[fn_][1])
    PTb = [S("PT%d" % i, [128, TT], BF16) for i in range(2)]; bPTb = [Buf(), Buf()]
    gsel, bgsel = SB("gsel", [128, 8, 8], F32)
    gcmp = dif[:].rearrange("p (a b c) -> p a b c", a=8, b=8); bgcmp = bdif
    grk, bgrk = SB("grk", [128, 8, 8], F32)
    xs = [praw[:, 0:512], F["kp"][0][:, :]]; bxs = [bpraw, F["kp"][1]]
    xout_v = x_out.rearrange("(c p) t -> p c t", p=128)
    wcnt = [0]

    def wchunk(ch):
        k = wcnt[0] % 2; wcnt[0] += 1
        P.dma("sp", wr[k][:], wb_dram[:, :, ch * 128:(ch + 1) * 128], w=[bwr[k]])
        return wr[k], bwr[k]

    def proj(ch, ps, bps):
        wt, bw = wchunk(ch)
        for c in range(8):
            P.op("pe", lambda e, c=c, wt=wt: e.matmul(ps[:], wt[:, c, :], xn[:, c, :], start=(c == 0), stop=(c == 7)),
                 r=[bw, bxn], w=[bps])

    def shift_mix(ch, ps, bps, dst, bdst, first):
        if first:
            P.op("pool", lambda e: e.memset(praw[:, 0:1], 0.0), w=[bpraw])
        else:
            P.op("pool", lambda e: e.tensor_copy(praw[:, 0:1], HP[:, ch:ch + 1]), r=[bHP], w=[bpraw])
        P.op("act", lambda e: e.copy(praw[:, 1:513], ps[:]), r=[bps], w=[bpraw])
        P.op("dve", lambda e: e.tensor_tensor(dif[:], praw[:, 0:512], praw[:, 1:513], op=ALU.subtract), r=[bpraw], w=[bdif])
        P.op("dve", lambda e: e.scalar_tensor_tensor(dst, dif[:], mu[:, ch:ch + 1], praw[:, 1:513], op0=ALU.mult, op1=ALU.add),
             r=[bdif, bpraw, bprm], w=[bdst])
        P.op("pool", lambda e: e.tensor_copy(HP[:, ch:ch + 1], praw[:, 512:513]), r=[bpraw], w=[bHP])

    for t in range(NT):
        b = t // 4; i = t % 4; first = (i == 0)
        for s in range(4):
            k = s % 2
            r0 = t * TT + s * 128
            P.dma("pool", xt[k][:], x_tok[r0:r0 + 128, :], w=[bxt[k]])
            for half in range(2):
                pb = 2 + half
                for cc in range(4):
                    c = half * 4 + cc
                    P.op("pe", lambda e, c=c, cc=cc, k=k, pb=pb: e.transpose(PB[pb][:, cc * 128:(cc + 1) * 128],
                                                                            xt[k][:, c * 128:(c + 1) * 128], C.ident[:]),
                         r=[bxt[k], C.b_ident], w=[bPB[pb]])
                P.op("act" if half == 0 else "dve",
                     (lambda e, s=s, pb=pb, half=half: e.copy(xf[:, half * 4:half * 4 + 4, s * 128:(s + 1) * 128],
                                                              PB[pb][:].rearrange("p (c f) -> p c f", c=4))) if half == 0 else
                     (lambda e, s=s, pb=pb, half=half: e.tensor_copy(xf[:, half * 4:half * 4 + 4, s * 128:(s + 1) * 128],
                                                                     PB[pb][:].rearrange("p (c f) -> p c f", c=4))),
                     r=[bPB[pb]], w=[bxf])
        rms_norm_tile(P, C, xf, bxf, prm, bprm, xn, bxn, sq, bsq, PB[0], bPB[0], rstd, brstd)
        proj(12, PB[1], bPB[1])
        shift_mix(12, PB[1], bPB[1], dif[:], bdif, first)
        P.op("act", lambda e: e.activation(tda[0:64, 0, :], dif[0:64, :], AF.Tanh), r=[bdif], w=[btda])
        P.op("act", lambda e: e.copy(tda[64:128, 1, :], dif[64:128, :]), r=[bdif], w=[btda])
        proj(13, PB[2], bPB[2])
        shift_mix(13, PB[2], bPB[2], dif[:], bdif, first)
        P.op("act", lambda e: e.activation(sdg[:], dif[:], AF.Sigmoid), r=[bdif], w=[bsdg])
        if stop == "C":
            return
        for pj in range(4):
            for (ch, gain, isq) in ((14 + pj, qn, True), (18 + pj, kn, False)):
                proj(ch, PB[1], bPB[1])
                P.op("act", lambda e: e.activation(sqk[:], PB[1][:], AF.Square), r=[bPB[1]], w=[bsqk])
                P.op("pe", lambda e: e.matmul(PB[2][:], BD, sqk[:], start=True, stop=True), r=[bcst, bsqk], w=[bPB[2]])
                rn_, brn_ = F["rn"]
                P.op("act", lambda e, rn_=rn_: e.activation(rn_[:], PB[2][:], AF.Sqrt, bias=1e-6, scale=1.0 / 64), r=[bPB[2]], w=[brn_])
                P.op("dve", lambda e, rn_=rn_: e.reciprocal(rn_[:], rn_[:]), r=[brn_], w=[brn_])
                if isq:
                    for hh in range(2):
                        hs = slice(64 * hh, 64 * hh + 64)
                        P.op("dve", lambda e, pj=pj, gain=gain, rn_=rn_, hh=hh, hs=hs: e.scalar_tensor_tensor(
                            QZ[hs, pj, hh, :], PB[1][hs, :], gain[hs, :], rn_[hs, :], op0=ALU.mult, op1=ALU.mult),
                             r=[bPB[1], brn_, bprm], w=[bQZ])
                else:
                    P.op("dve", lambda e, pj=pj, gain=gain, rn_=rn_, i=i: e.scalar_tensor_tensor(KT[:, pj, i * TT:(i + 1) * TT], PB[1][:], gain,
                                                                                               rn_[:], op0=ALU.mult, op1=ALU.mult),
                         r=[bPB[1], brn_, bprm], w=[bKT])
                    P.op("dve", lambda e, pj=pj, i=i: e.tensor_reduce(kmf[:, pj, 2 * i:2 * i + 2],
                                                                      KT[:, pj, i * TT:(i + 1) * TT].rearrange("p (b t) -> p b t", b=2),
                                                                      axis=AX.X, op=ALU.add), r=[bKT], w=[bkmf])
        P.op("act", lambda e: e.copy(kmT[:], kmf[:]), r=[bkmf], w=[bkmT])
        if stop == "C2":
            return
        P.dma("sp", wvv[:], wb_dram[:, :, 2816:3328], w=[bwvv])
        for s in range(4):
            pb = 1 + (s % 2)
            for c in range(8):
                P.op("pe", lambda e, c=c, s=s, pb=pb: e.matmul(PB[pb][:], xn[:, c, s * 128:(s + 1) * 128], wvv[:, c, :],
                                                               start=(c == 0), stop=(c == 7)), r=[bxn, bwvv], w=[bPB[pb]])
            P.op("act", lambda e, s=s, pb=pb, i=i: e.copy(VC[:, i * 4 + s, :], PB[pb][:]), r=[bPB[pb]], w=[bVC])
        if stop == "C3":
            return
        for s in range(4):
            n = 2 * i + s // 2
            for h in range(8):
                P.op("pe", lambda e, h=h, s=s: e.matmul(PB[1][:, h * 64:h * 64 + 8], QZ[:, h // 2, h % 2, s * 128:(s + 1) * 128],
                                                       kmT[:, h // 2, :], start=True, stop=True),
                     r=[bQZ, bkmT], w=[bPB[1]])
            if stop == "D1":
                return
            past = bcast(pm[:, 0, n, :], 1, 8); own = bcast(pm[:, 1, n, :], 1, 8); negm = bcast(pm[:, 2, n, :], 1, 8)
            P.op("dve", lambda e, past=past: e.tensor_tensor(gsel[:], PB[1][:].rearrange("p (h j) -> p h j", h=8)[:, :, 0:8], past, op=ALU.mult),
                 r=[bPB[1], bpm], w=[bgsel])
            P.op("dve", lambda e, negm=negm: e.tensor_tensor(gsel[:], gsel[:], negm, op=ALU.add), r=[bgsel, bpm], w=[bgsel])
            P.op("dve", lambda e: e.tensor_tensor(gcmp, bcast(gsel[:], 2, 8), bcast(gsel[:], 3, 8), op=ALU.is_gt),
                 r=[bgsel], w=[bgcmp])
            P.op("dve", lambda e: e.tensor_reduce(grk[:], gcmp, axis=AX.X, op=ALU.add), r=[bgcmp], w=[bgrk])
            P.op("dve", lambda e: e.tensor_single_scalar(grk[:], grk[:], 3.0, op=ALU.is_lt), r=[bgrk], w=[bgrk])
            P.op("dve", lambda e, past=past: e.tensor_tensor(grk[:], grk[:], past, op=ALU.mult), r=[bgrk, bpm], w=[bgrk])
            P.op("dve", lambda e, own=own: e.tensor_tensor(grk[:], grk[:], own, op=ALU.add), r=[bgrk, bpm], w=[bgrk])
            P.op("dve", lambda e: e.tensor_scalar(grk[:], grk[:], -1.0, 30000.0, op0=ALU.add, op1=ALU.mult), r=[bgrk], w=[bgrk])
            if stop == "D2":
                return
            P.op("pe", lambda e: e.transpose(PB[2][0:64, 0:128], grk[:].rearrange("p h j -> p (h j)"), C.ident[:]),
                 r=[bgrk, C.b_ident], w=[bPB[2]])
            P.op("act", lambda e, s=s: e.copy(NMT[:, s * 128:(s + 1) * 128], PB[2][0:64, 0:128]), r=[bPB[2]], w=[bNMT])

        if stop == "D":
            return
        def rw_pair(j):
            r_, br = F["r"]; k_, bk = F["k"]; v_, bv = F["v"]; sg, bsg = F["sg"]; a_, ba = F["a"]; lcs, blcs = F["lcs"]
            t1, bt1 = F["t1"]; e1, be1 = F["e1"]; e2, be2 = F["e2"]; e3, be3 = F["e3"]; kk0, bkk0 = F["kk0"]; rn_, brn_ = F["rn"]
            kp, bkp = F["kp"]
            for (ch, dst, bd) in ((j, r_, br), (4 + j, k_, bk), (8 + j, v_, bv)):
                proj(ch, PB[1], bPB[1])
                shift_mix(ch, PB[1], bPB[1], dst[:], bd, first)
                yield
            P.op("pe", lambda e: e.matmul(PB[1][:], lwa[:, j * 128:(j + 1) * 128], tda[:, 0, :], start=True, stop=True),
                 r=[blwa, btda], w=[bPB[1]])
            P.op("act", lambda e: e.activation(sg[:], PB[1][:], AF.Sigmoid, bias=w0[:, j:j + 1]), r=[bPB[1], bprm], w=[bsg])
            P.op("pe", lambda e: e.matmul(PB[2][:], lwa[:, j * 128:(j + 1) * 128], tda[:, 1, :], start=True, stop=True),
                 r=[blwa, btda], w=[bPB[2]])
            P.op("act", lambda e: e.activation(a_[:], PB[2][:], AF.Sigmoid, bias=a0[:, j:j + 1]), r=[bPB[2], bprm], w=[ba])
            P.op("dve", lambda e: e.tensor_tensor_scan(lcs[:], MS[:], sg[:], 0.0, op0=ALU.mult, op1=ALU.add), r=[bMS, bsg], w=[blcs])
            P.op("pool", lambda e: e.tensor_tensor(t1[:], lcs[:], sg[:], op=ALU.subtract), r=[blcs, bsg], w=[bt1])
            P.op("act", lambda e: e.activation(e1[:], t1[:], AF.Exp, scale=-CDEC), r=[bt1], w=[be1])
            P.op("act", lambda e: e.activation(e2[:], lcs[:], AF.Exp, scale=-CDEC), r=[blcs], w=[be2])
            P.op("act", lambda e: e.activation(e3[:], lcs[:], AF.Exp, scale=CDEC), r=[blcs], w=[be3])
            P.op("pool", lambda e: e.tensor_copy(GC[:], e2[:].rearrange("p (c t) -> p c t", t=64)[:, :, 63]), r=[be2], w=[bGC])
            yield
            P.op("dve", lambda e: e.tensor_scalar(kk0[:], k_[:], kk_[:, j:j + 1], None, op0=ALU.mult), r=[bk, bprm], w=[bkk0])
            P.op("act", lambda e: e.activation(sqk[:], kk0[:], AF.Square), r=[bkk0], w=[bsqk])
            P.op("pe", lambda e: e.matmul(PB[1][:], BD, sqk[:], start=True, stop=True), r=[bcst, bsqk], w=[bPB[1]])
            P.op("act", lambda e: e.activation(rn_[:], PB[1][:], AF.Sqrt, bias=1e-24), r=[bPB[1]], w=[brn_])
            P.op("dve", lambda e: e.reciprocal(rn_[:], rn_[:]), r=[brn_], w=[brn_])
            P.op("dve", lambda e: e.tensor_tensor(kk0[:], kk0[:], rn_[:], op=ALU.mult), r=[bkk0, brn_], w=[bkk0])
            P.op("dve", lambda e: e.tensor_scalar(kp[:], a_[:], -1.0, ka_[:, j:j + 1], op0=ALU.add, op1=ALU.mult), r=[ba, bprm], w=[bkp])
            P.op("dve", lambda e: e.scalar_tensor_tensor(kp[:], kp[:], 1.0, k_[:], op0=ALU.add, op1=ALU.mult), r=[bkp, bk], w=[bkp])
            for hh in range(2):
                hs = slice(64 * hh, 64 * hh + 64)
                P.op("dve", lambda e, hh=hh, hs=hs: e.scalar_tensor_tensor(AR[hs, hh, 0, :], kk0[hs, :], -1.0, e1[hs, :], op0=ALU.mult, op1=ALU.mult),
                     r=[bkk0, be1], w=[bAR])
                P.op("dve", lambda e, hh=hh, hs=hs: e.tensor_tensor(AR[hs, hh, 1, :], r_[hs, :], e2[hs, :], op=ALU.mult), r=[br, be2], w=[bAR])
            P.op("dve", lambda e: e.tensor_tensor(t1[:], kk0[:], a_[:], op=ALU.mult), r=[bkk0, ba], w=[bt1])
            P.op("dve", lambda e: e.tensor_tensor(BKV[:, 0, :], t1[:], e3[:], op=ALU.mult), r=[bt1, be3], w=[bBKV])
            P.op("dve", lambda e: e.tensor_tensor(BKV[:, 1, :], kp[:], e3[:], op=ALU.mult), r=[bkp, be3], w=[bBKV])
            P.op("act", lambda e: e.copy(BKV[:, 2, :], v_[:]), r=[bv], w=[bBKV])
            for hh in range(2):
                hs = slice(64 * hh, 64 * hh + 64)
                P.op("dve", lambda e, hh=hh, hs=hs: e.scalar_tensor_tensor(rkp[hs, hh, :], r_[hs, :], rk_[hs, j:j + 1], kp[hs, :], op0=ALU.mult, op1=ALU.mult),
                     r=[br, bkp, bprm], w=[brkp])
            if first:
                P.op("pool", lambda e: e.memset(ST[j][:], 0.0), w=[bST[j]])
                P.op("pool", lambda e: e.memset(STb[j][:], 0.0), w=[bSTb[j]])
            yield
            pa = PB[3][:].rearrange("p (k h c) -> p k h c", k=2, h=2)
            pn = PB[4][:, 0:128].rearrange("p (h c) -> p h c", h=2)
            pi = PB[5][:].rearrange("p (h c) -> p h c", h=2)
            ptb = PB[6][:].bitcast(BF16)
            px = PB[7][:, 0:128].rearrange("p (h c) -> p h c", h=2)
            pu = PB[7][:, 128:256].rearrange("p (h c) -> p h c", h=2)
            py = PB[7][:, 256:384].rearrange("p (h c) -> p h c", h=2)
            pp = PB[4][:, 256:384]

            def prep(c):
                s_ = c % 2
                cs = slice(c * 64, (c + 1) * 64)
                Ws, bWs = WlS[s_], bWlS[s_]
                AMc, bAMc = AMS[s_], bAMS[s_]
                BKc, bBKc = BKVtS[s_], bBKVtS[s_]
                for hh in range(2):
                    P.op("pe", lambda e, hh=hh: e.matmul(pa[0:64, 0, hh, :], BKV[:, 0, cs], AR[:, hh, :, cs], start=True, stop=True),
                         r=[bBKV, bAR], w=[bPB[3]])
                    P.op("pe", lambda e, hh=hh: e.matmul(pa[0:64, 1, hh, :], BKV[:, 1, cs], AR[:, hh, :, cs], start=True, stop=True),
                         r=[bBKV, bAR], w=[bPB[3]])
                    P.op("pe", lambda e, hh=hh: e.matmul(pn[0:64, hh, 0:64], AR[:, hh, 0, cs], BKV[:, 0, cs], start=True, stop=True),
                         r=[bBKV, bAR], w=[bpn, bPB[4]])
                W0 = Ws[0]
                P.op("dve", lambda e: e.tensor_tensor(W0[:, :, 0, :], pa[0:64, 0, :, 0:64], MSU, op=ALU.mult), r=[bPB[3], bmk], w=[bWs[0]])
                P.op("dve", lambda e: e.tensor_tensor(W0[:, :, 2, :], pn[0:64, :, 0:64], MSL, op=ALU.mult), r=[bpn, bmk], w=[bWs[0]])
                P.op("pool", lambda e: e.tensor_copy(W0[:, :, 1, :], bcast(cst[0:64, 0, 0:64], 1, 2)), r=[bcst], w=[bWs[0]])
                P.op("dve", lambda e: e.tensor_tensor(AMc[:, :, 0, :], pa[0:64, 0, :, 64:128], MU, op=ALU.mult), r=[bPB[3], bmk], w=[bAMc])
                P.op("dve", lambda e: e.tensor_tensor(AMc[:, :, 1, :], pa[0:64, 1, :, 0:64], MSU, op=ALU.mult), r=[bPB[3], bmk], w=[bAMc])
                P.op("dve", lambda e: e.tensor_tensor(AMc[:, :, 2, :], pa[0:64, 1, :, 64:128], MU, op=ALU.mult), r=[bPB[3], bmk], w=[bAMc])
                for q in range(3):
                    P.op("pe", lambda e, q=q: e.transpose(ptb[0:64, q * 128:(q + 1) * 128], BKV[:, q, cs], identb),
                         r=[bBKV, bcst], w=[bPB[6]])
                P.op("act", lambda e: e.copy(BKc[:].rearrange("p q c -> p (q c)"), ptb[0:64, 0:384]), r=[bPB[6]], w=[bBKc])
                yield
                for lvl in range(6):
                    Wc, bWc = Ws[lvl % 2], bWs[lvl % 2]
                    Wn, bWn = Ws[(lvl + 1) % 2], bWs[(lvl + 1) % 2]
                    for hh in range(2):
                        P.op("pe", lambda e, hh=hh, Wc=Wc: e.matmul(pi[0:64, hh, 0:128], Wc[:, hh, 2, :],
                                                                    Wc[:, hh, 0:2, :], start=True, stop=True), r=[bWc], w=[bPB[5]])
                        if lvl < 5:
                            P.op("pe", lambda e, hh=hh, Wc=Wc: e.matmul(pi[0:64, hh, 128:192], Wc[:, hh, 0, :], Wc[:, hh, 2, :],
                                                                        start=True, stop=True), r=[bWc], w=[bPB[5]])
                    if lvl < 5:
                        P.op("act", lambda e, Wn=Wn: e.copy(Wn[:, :, 0, :], pi[0:64, :, 0:64]), r=[bPB[5]], w=[bWn])
                        P.op("act", lambda e, Wn=Wn: e.copy(Wn[:, :, 2, :], pi[0:64, :, 128:192]), r=[bPB[5]], w=[bWn])
                    P.op("dve", lambda e, Wn=Wn, Wc=Wc: e.tensor_tensor(Wn[:, :, 1, :], pi[0:64, :, 64:128], Wc[:, :, 1, :], op=ALU.add),
                         r=[bPB[5], bWc], w=[bWn])
                    yield

            def seq(c):
                s_ = c % 2
                cs = slice(c * 64, (c + 1) * 64)
                Wf, bWf = WlS[s_][0], bWlS[s_][0]
                AMc, bAMc = AMS[s_], bAMS[s_]
                BKc, bBKc = BKVtS[s_], bBKVtS[s_]
                for hh in range(2):
                    hs = slice(64 * hh, 64 * hh + 64)
                    P.op("pe", lambda e, hh=hh, hs=hs: e.matmul(px[0:64, hh, :], AR[:, hh, 0, cs], STb[j][:, hs], start=True, stop=False),
                         r=[bAR, bSTb[j]], w=[bPB[7]])
                    P.op("pe", lambda e, hh=hh, hs=hs: e.matmul(px[0:64, hh, :], AMc[:, hh, 1, :], BKc[:, 2, hs], start=False, stop=True),
                         r=[bAMc, bBKc], w=[bPB[7]])
                P.op("act", lambda e: e.copy(Xb[:], px[0:64, :, :]), r=[bPB[7]], w=[bXb])
                yield
                for hh in range(2):
                    P.op("pe", lambda e, hh=hh: e.matmul(pu[0:64, hh, :], Wf[:, hh, 1, :], Xb[:, hh, :], start=True, stop=True),
                         r=[bWf, bXb], w=[bPB[7]])
                P.op("act", lambda e: e.copy(Ub[:], pu[0:64, :, :]), r=[bPB[7]], w=[bUb])
                yield
                for hh in range(2):
                    hs = slice(64 * hh, 64 * hh + 64)
                    P.op("pe", lambda e, hh=hh, hs=hs: e.matmul(py[0:64, hh, :], STb[j][:, hs], AR[:, hh, 1, cs], start=True, stop=False),
                         r=[bAR, bSTb[j]], w=[bPB[7]])
                    P.op("pe", lambda e, hh=hh: e.matmul(py[0:64, hh, :], Ub[:, hh, :], AMc[:, hh, 0, :], start=False, stop=False),
                         r=[bAMc, bUb], w=[bPB[7]])
                    P.op("pe", lambda e, hh=hh, hs=hs: e.matmul(py[0:64, hh, :], BKc[:, 2, hs], AMc[:, hh, 2, :], start=False, stop=True),
                         r=[bAMc, bBKc], w=[bPB[7]])
                P.op("act", lambda e: e.copy(YT[:, :, cs], py[0:64, :, :]), r=[bPB[7]], w=[bYT])
                P.op("pe", lambda e: e.matmul(pp, BKc[:, 0, :], Ub[:].rearrange("p h c -> p (h c)"), start=True, stop=False),
                     r=[bBKc, bUb], w=[bpp, bPB[4]])
                P.op("pe", lambda e: e.matmul(pp, BKc[:, 1, :], BKc[:, 2, :], start=False, stop=True), r=[bBKc], w=[bpp, bPB[4]])
                P.op("dve", lambda e: e.tensor_tensor(tS[:], pp, ST[j][:], op=ALU.add), r=[bpp, bST[j]], w=[btS])
                P.op("dve", lambda e: e.tensor_scalar(ST[j][:], tS[:], GC[:, c:c + 1], None, op0=ALU.mult), r=[btS, bGC], w=[bST[j]])
                P.op("act", lambda e: e.copy(STb[j][:], ST[j][:]), r=[bST[j]], w=[bSTb[j]])
                yield

            for _ in prep(0):
                yield
            for c in range(8):
                gl = [seq(c)] + ([prep(c + 1)] if c + 1 < 8 else [])
                while gl:
                    for g_ in list(gl):
                        try:
                            next(g_)
                        except StopIteration:
                            gl.remove(g_)
                    yield
            mean, bmean = G["mean"]; msq, bmsq = G["msq"]; var, bvar = G["var"]; cen, bcen = G["cen"]; ysq, bysq = G["ysq"]
            vs, bvs = G["vs"]; bon, bbon = G["bon"]
            for hh in range(2):
                h = 2 * j + hh
                hs = slice(64 * hh, 64 * hh + 64)
                y_ = YT[:, hh, :]
                P.op("act", lambda e, y_=y_: e.activation(ysq[:], y_, AF.Square), r=[bYT], w=[bysq])
                P.op("pe", lambda e, y_=y_: e.matmul(PB[1][0:64, :], ones64f[:], y_, start=True, stop=True), r=[bo64, bYT], w=[bPB[1]])
                P.op("pe", lambda e: e.matmul(PB[2][0:64, :], ones64f[:], ysq[:], start=True, stop=True), r=[bo64, bysq], w=[bPB[2]])
                P.op("act", lambda e: e.mul(mean[:], PB[1][0:64, :], 1.0 / 64), r=[bPB[1]], w=[bmean])
                P.op("dve", lambda e: e.tensor_tensor(msq[:], mean[:], mean[:], op=ALU.mult), r=[bmean], w=[bmsq])
                P.op("dve", lambda e: e.scalar_tensor_tensor(var[:], PB[2][0:64, :], 1.0 / 64, msq[:], op0=ALU.mult, op1=ALU.subtract),
                     r=[bPB[2], bmsq], w=[bvar])
                P.op("act", lambda e: e.activation(var[:], var[:], AF.Sqrt, bias=64e-5), r=[bvar], w=[bvar])
                P.op("dve", lambda e: e.reciprocal(var[:], var[:]), r=[bvar], w=[bvar])
                P.op("dve", lambda e, y_=y_: e.tensor_tensor(cen[:], y_, mean[:], op=ALU.subtract), r=[bYT, bmean], w=[bcen])
                P.op("dve", lambda e: e.tensor_tensor(cen[:], cen[:], var[:], op=ALU.mult), r=[bcen, bvar], w=[bcen])
                P.op("act", lambda e, h=h: e.activation(cen[:], cen[:], AF.Identity, bias=lnb[:, h:h + 1], scale=lnw[:, h:h + 1]),
                     r=[bcen, bprm], w=[bcen])
                P.op("pe", lambda e, hs=hs, hh=hh: e.matmul(PB[1][0:64, :], C.ones_bf[:, 0:64], rkp[:, hh, :], start=True, stop=True),
                     r=[C.b_ones, brkp], w=[bPB[1]])
                P.op("pe", lambda e, hs=hs: e.matmul(PB[2][0:64, :], cst[:, 0, hs], BKV[:, 2, :], start=True, stop=True),
                     r=[bcst, bBKV], w=[bPB[2]])
                P.op("act", lambda e: e.copy(vs[:], PB[2][0:64, :]), r=[bPB[2]], w=[bvs])
                P.op("dve", lambda e: e.tensor_tensor(bon[:], PB[1][0:64, :], vs[:], op=ALU.mult), r=[bPB[1], bvs], w=[bbon])
                P.op("dve", lambda e: e.tensor_tensor(cen[:], cen[:], bon[:], op=ALU.add), r=[bcen, bbon], w=[bcen])
                P.op("pe", lambda e, h=h: e.matmul(PB[1][0:64, :], glo[:, h * 64:(h + 1) * 64], sdg[:], start=True, stop=True),
                     r=[bglo, bsdg], w=[bPB[1]])
                P.op("dve", lambda e, h=h: e.tensor_tensor(yall[:, h, :], cen[:], PB[1][0:64, :], op=ALU.mult), r=[bcen, bPB[1]], w=[byall])
                yield

        for j in range(4):
            for _ in rw_pair(j):
                pass
            if stop == "E":
                return

        rden, brden = G["rden"]
        for h in range(8):
            hp = 64 * (h % 2); pj = h // 2; hs = slice(hp, hp + 64)
            tz = tzb[h % 2]; bz = btz[h % 2]
            P.dma("pool", tz[:], tz_dram[h], w=[bz])
            nkt = 4 * i + 4
            for kt in range(nkt):
                k2 = kt % 2
                sc, bsc = PB[1 + k2], bPB[1 + k2]
                m0 = (i * 512 - kt * 128) + 384
                r_ = 8 * h + kt // 2
                P.op("pe", lambda e, hs=hs, pj=pj, kt=kt, sc=sc, h=h: e.matmul(sc[:], KT[:, pj, kt * 128:(kt + 1) * 128], QZ[:, pj, h % 2, :],
                                                                         start=True, stop=False), r=[bKT, bQZ], w=[bsc])
                P.op("pe", lambda e, r_=r_, sc=sc: e.matmul(sc[:], bcast(cst[0:64, 0, r_:r_ + 1], 1, 128)[:, :, 0] if False else
                                                            bass.AP(cst.tensor if hasattr(cst, "tensor") else cst, cst[0:64, 0, r_:r_ + 1].offset,
                                                                    [list(cst[0:64, 0, r_:r_ + 1].ap[0]), [0, 128]]),
                                                            NMT[:], start=False, stop=True), r=[bcst, bNMT], w=[bsc])
                P.op("act", lambda e, k2=k2, sc=sc: e.activation(PTb[k2][:], sc[:], AF.Exp, scale=0.125), r=[bsc], w=[bPTb[k2]])
                P.op("dve", lambda e, k2=k2, tz=tz, m0=m0: e.tensor_tensor(PTb[k2][:], PTb[k2][:], tz[:, m0:m0 + 512], op=ALU.mult),
                     r=[bPTb[k2], bz], w=[bPTb[k2]])
                P.op("pe", lambda e, k2=k2, kt=kt, h=h: e.matmul(PB[3][0:64, :], VC[:, kt, h * 64:(h + 1) * 64], PTb[k2][:],
                                                                 start=(kt == 0), stop=(kt == nkt - 1)), r=[bVC, bPTb[k2]], w=[bPB[3]])
                P.op("pe", lambda e, k2=k2, kt=kt: e.matmul(PB[4][0:64, :], C.ones_bf[:, 0:64], PTb[k2][:],
                                                            start=(kt == 0), stop=(kt == nkt - 1)), r=[C.b_ones, bPTb[k2]], w=[bPB[4], bpn, bpp])
            P.op("dve", lambda e: e.reciprocal(rden[:], PB[4][0:64, :]), r=[bPB[4]], w=[brden])
            P.op("dve", lambda e, h=h: e.tensor_tensor(yall[:, 8 + h, :], PB[3][0:64, :], rden[:], op=ALU.mult), r=[bPB[3], brden], w=[byall])
        if stop == "F":
            return
        for m in range(8):
            k = m % 2
            pd, bpd = PB[5 + k], bPB[5 + k]
            for h in range(16):
                P.op("pe", lambda e, m=m, h=h, pd=pd: e.matmul(pd[:], wout[:, h, m * 128:(m + 1) * 128], yall[:, h, :],
                                                               start=(h == 0), stop=(h == 15)), r=[bwout, byall], w=[bpd])
            P.op("dve", lambda e, m=m, k=k, pd=pd: e.tensor_tensor(xs[k], pd[:], xf[:, m, :], op=ALU.add), r=[bpd, bxf], w=[bxs[k]])
            P.dma("sp", xout_v[:, m, t * TT:(t + 1) * TT], xs[k], r=[bxs[k]])


def make_t5oh():
    import jax, jax.numpy as jnp, math
    cpu = jax.devices("cpu")[0]
    with jax.default_device(cpu):
        dist = jnp.arange(NI, dtype=jnp.int32) - 511
        n = jnp.maximum(dist, 0)
        nf = jnp.maximum(n, 1).astype(jnp.float32)
        large = 16 + (jnp.log(nf / 16) / math.log(1024 / 16) * 16).astype(jnp.int32)
        bucket = np.asarray(jnp.where(n < 16, n, jnp.minimum(large, 31)))
        dist = np.asarray(dist)
    oh = np.zeros((33, NI), np.float32)
    for i_ in range(NI):
        if dist[i_] < 0:
            oh[32, i_] = 1.0
        else:
            oh[bucket[i_], i_] = 1.0
    return oh


from concourse.bass_utils import run_bass_kernel_spmd

W_NAMES = ["rel_bias",
           "l0_mix_norm", "l0_w_in", "l0_shift_mu", "l0_w0", "l0_w_lora_up", "l0_a0", "l0_a_lora_up", "l0_g_lora_up",
           "l0_k_k", "l0_k_a", "l0_r_k", "l0_lnx_w", "l0_lnx_b", "l0_q_norm", "l0_k_norm", "l0_w_out",
           "l0_ffn_norm", "l0_ffn_up", "l0_ffn_conv_w", "l0_ffn_conv_b", "l0_ffn_down",
           "l1_mix_norm", "l1_w_in", "l1_conv_w", "l1_w_out",
           "l1_ffn_norm", "l1_ffn_up", "l1_ffn_conv_w", "l1_ffn_conv_b", "l1_ffn_down"]


def build_program(shapes):
    nc = bass.Bass("TRN2", target_bir_lowering=False)
    P = Prog(nc); C = Ctx()
    x_tok = nc.dram_tensor("x_tok", [TOK, D], F32, kind="ExternalInput").ap()
    out_tok = nc.dram_tensor("out_tok", [TOK, D], F32, kind="ExternalOutput").ap()
    t5oh = nc.dram_tensor("t5oh", [33, NI], F32, kind="ExternalInput").ap()
    W = {n: nc.dram_tensor(n, list(shapes[n]), F32, kind="ExternalInput").ap() for n in W_NAMES}
    wb = nc.dram_tensor("wb_scr", [128, 8, 3328], BF16).ap()
    ev = nc.dram_tensor("ev_scr", [8, NI], BF16).ap()
    tz = nc.dram_tensor("tz_scr", [8, 128, EW], BF16).ap()
    xa = nc.dram_tensor("xa_scr", [D, TOK], F32).ap()
    xb = nc.dram_tensor("xb_scr", [D, TOK], F32).ap()
    xc = nc.dram_tensor("xc_scr", [D, TOK], F32).ap()
    mk_consts(P, C)
    P.barrier()
    glob = P.es

    def phase(fn):
        st = ExitStack(); P.es = st
        fn()
        st.close(); P.es = glob
        P.barrier()

    phase(lambda: phase_l0(P, C, x_tok, xa, W, t5oh, wb, ev, tz, "a0"))
    phase(lambda: phase_ffn(P, C, xa, xb, None, [W[n] for n in W_NAMES[17:22]], "f0"))
    phase(lambda: phase_l1mix(P, C, xb, xc, [W[n] for n in W_NAMES[22:26]], "m1"))
    phase(lambda: phase_ffn(P, C, xc, None, out_tok, [W[n] for n in W_NAMES[26:31]], "f1"))
    P.emit()
    return nc


def kernel(**inputs):
    x = np.ascontiguousarray(np.asarray(inputs["x"], dtype=np.float32))
    B = x.shape[0]
    ncores = 8
    per = B // ncores
    shapes = {n: np.asarray(inputs[n]).shape for n in W_NAMES}
    nc = build_program(shapes)
    oh = make_t5oh()
    in_maps = []
    for c in range(ncores):
        m = {"x_tok": np.ascontiguousarray(x[c * per:(c + 1) * per].reshape(TOK, D)), "t5oh": oh}
        for n in W_NAMES:
            m[n] = np.ascontiguousarray(np.asarray(inputs[n], dtype=np.float32))
        in_maps.append(m)
    res = run_bass_kernel_spmd(nc, in_maps, core_ids=list(range(ncores)))
    outs = [np.asarray(r["out_tok"]).reshape(per, T, D) for r in res.results]
    return np.concatenate(outs, axis=0).astype(np.float32)
```
